# Optimizing a Trainium2 kernel written in Bass

```python
import math
import jax, jax.numpy as jnp
from jax import lax
import numpy as np

D_MODEL = 1024
BATCH = 2
SEQ = 8192
DEPTH = 2

MEM_LEN = 256
N_MIXERS = 4
D_GROUP = D_MODEL // N_MIXERS
N_IN_COLS = 6 * D_GROUP
S5_CH_PER_GROUP = 16
S5_GROUPS = D_GROUP // S5_CH_PER_GROUP
S5_STATE = 64
S5_DT_MIN = 1e-3
S5_DT_MAX = 1e-1
CONV_WIDTH = 31
CONV_GN_GROUPS = 4
LRU_HEADS = 4
LRU_HEAD_DIM = D_GROUP // LRU_HEADS
LRU_CONV_WIDTH = 4
LRU_C = 8.0
ATTN_HEADS = 4
ATTN_HEAD_DIM = D_GROUP // ATTN_HEADS
D_FF = ((8 * D_MODEL // 3 + 127) // 128) * 128
FFN_CONV_WIDTH = 3
DEEPNORM_ALPHA = (2 * DEPTH) ** 0.25
DEEPNORM_BETA = (8 * DEPTH) ** -0.25
LN_EPS = 1e-5

kernel_name = "hymba_style_s5_conformer_rglru_memxattn_deepnorm"

F32 = jnp.float32


def layer_norm(x, g, b):
    xf = x.astype(F32)
    mu = jnp.mean(xf, -1, keepdims=True)
    var = jnp.mean(jnp.square(xf - mu), -1, keepdims=True)
    return ((xf - mu) * lax.rsqrt(var + LN_EPS) * g.astype(F32) + b.astype(F32)).astype(x.dtype)


def group_norm(x, g, b, groups):
    lead = x.shape[:-1]
    c = x.shape[-1]
    xf = x.astype(F32).reshape(lead + (groups, c // groups))
    mu = jnp.mean(xf, -1, keepdims=True)
    var = jnp.mean(jnp.square(xf - mu), -1, keepdims=True)
    xn = ((xf - mu) * lax.rsqrt(var + LN_EPS)).reshape(lead + (c,))
    return xn * g.astype(F32) + b.astype(F32)


def causal_dwconv(x, w, b):
    k, c = w.shape
    y = lax.conv_general_dilated(
        x, w.astype(x.dtype)[:, None, :], window_strides=(1,), padding=[(k - 1, 0)],
        dimension_numbers=("NWC", "WIO", "NWC"), feature_group_count=c)
    return y + b.astype(x.dtype)


def linear_scan(a, b):
    def combine(l, r):
        a1, b1 = l
        a2, b2 = r
        return a1 * a2, a2 * b1 + b2
    _, h = lax.associative_scan(combine, (a, b), axis=1)
    return h


def s5_mixer(u, lam_re, lam_im, log_dt, b_re, b_im, c_re, c_im, d_skip, w_glu, b_glu):
    lead = u.shape[:-1]
    uf = u.astype(F32).reshape(lead + (S5_GROUPS, S5_CH_PER_GROUP))
    lam = lax.complex(lam_re.astype(F32), lam_im.astype(F32))
    dt = jnp.exp(log_dt.astype(F32))[:, None]
    lam_bar = jnp.exp(lam * dt)
    bmat = lax.complex(b_re.astype(F32), b_im.astype(F32))
    b_bar = ((lam_bar - 1.0) / lam)[..., None] * bmat
    bu = jnp.einsum("bsgc,gpc->bsgp", uf, b_bar)
    states = linear_scan(jnp.broadcast_to(lam_bar, bu.shape), bu)
    cmat = lax.complex(c_re.astype(F32), c_im.astype(F32))
    y = jnp.einsum("bsgp,gcp->bsgc", states, cmat).real
    y = y + d_skip.astype(F32).reshape(S5_GROUPS, S5_CH_PER_GROUP) * uf
    y = jax.nn.gelu(y.reshape(lead + (D_GROUP,)))
    return y * jax.nn.sigmoid(y @ w_glu.astype(F32) + b_glu.astype(F32))


def conformer_conv(v, g, conv_w, conv_b, gn_g, gn_b, w_pw, b_pw):
    h = v * jax.nn.sigmoid(g)
    h = causal_dwconv(h, conv_w, conv_b)
    h = jax.nn.silu(group_norm(h, gn_g, gn_b, CONV_GN_GROUPS))
    return h @ w_pw.astype(F32) + b_pw.astype(F32)


def rglru_branch(xg, xr, conv_w, conv_b, w_r, b_r, w_i, b_i, lam):
    gate = jax.nn.gelu(xg.astype(F32))
    xc = causal_dwconv(xr, conv_w, conv_b).astype(F32)
    lead = xc.shape[:-1]
    xh = xc.reshape(lead + (LRU_HEADS, LRU_HEAD_DIM))
    r = jax.nn.sigmoid(jnp.einsum("bshi,hij->bshj", xh, w_r.astype(F32)).reshape(lead + (D_GROUP,)) + b_r.astype(F32))
    i = jax.nn.sigmoid(jnp.einsum("bshi,hij->bshj", xh, w_i.astype(F32)).reshape(lead + (D_GROUP,)) + b_i.astype(F32))
    log_a = -LRU_C * r * jax.nn.softplus(-lam.astype(F32))
    a = jnp.exp(log_a)
    bvals = jnp.sqrt(-jnp.expm1(2.0 * log_a)) * (i * xc)
    h = linear_scan(a, bvals)
    return h * gate


def memory_cross_attention(q, mem, w_kv):
    lead = q.shape[:-1]
    qh = q.astype(F32).reshape(lead + (ATTN_HEADS, ATTN_HEAD_DIM))
    kv = mem.astype(F32) @ w_kv.astype(F32)
    k, v = jnp.split(kv, 2, axis=-1)
    k = k.reshape(k.shape[:-1] + (ATTN_HEADS, ATTN_HEAD_DIM))
    v = v.reshape(v.shape[:-1] + (ATTN_HEADS, ATTN_HEAD_DIM))
    s = jnp.einsum("bshd,bmhd->bhsm", qh, k) * (ATTN_HEAD_DIM ** -0.5)
    p = jax.nn.softmax(s, axis=-1)
    o = jnp.einsum("bhsm,bmhd->bshd", p, v)
    return o.reshape(lead + (D_GROUP,))


def conv_ffn(x, w_up, conv_w, conv_b, w_down):
    u = x @ w_up
    u = causal_dwconv(u, conv_w, conv_b)
    val, gt = jnp.split(u, 2, axis=-1)
    h = val.astype(F32) * jax.nn.gelu(gt.astype(F32))
    return h @ w_down.astype(F32)


def setup_inputs(seed: int = 0) -> dict:
    key = jax.random.key(seed)
    keys = iter(jax.random.split(key, 64))
    L = DEPTH

    def nrm(shape, scale):
        return jax.random.normal(next(keys), shape, F32) * scale

    def gain(shape):
        return 1.0 + nrm(shape, 0.02)

    d = {}
    d["x"] = nrm((BATCH, SEQ, D_MODEL), 1.0)
    d["mem"] = nrm((BATCH, MEM_LEN, D_MODEL), 1.0)
    d["ln_in_g"] = gain((D_MODEL,))
    d["ln_in_b"] = nrm((D_MODEL,), 0.02)
    d["w_in"] = nrm((L, D_MODEL, N_IN_COLS), D_MODEL ** -0.5)
    d["b_in"] = nrm((L, N_IN_COLS), 0.01)
    d["s5_lam_re"] = -0.5 + nrm((L, S5_GROUPS, S5_STATE), 0.01)
    d["s5_lam_im"] = math.pi * jnp.arange(S5_STATE, dtype=F32) + nrm((L, S5_GROUPS, S5_STATE), 0.01)
    d["s5_log_dt"] = jax.random.uniform(next(keys), (L, S5_GROUPS), F32, math.log(S5_DT_MIN), math.log(S5_DT_MAX))
    d["s5_b_re"] = nrm((L, S5_GROUPS, S5_STATE, S5_CH_PER_GROUP), (2 * S5_CH_PER_GROUP) ** -0.5)
    d["s5_b_im"] = nrm((L, S5_GROUPS, S5_STATE, S5_CH_PER_GROUP), (2 * S5_CH_PER_GROUP) ** -0.5)
    d["s5_c_re"] = nrm((L, S5_GROUPS, S5_CH_PER_GROUP, S5_STATE), (2 * S5_STATE) ** -0.5)
    d["s5_c_im"] = nrm((L, S5_GROUPS, S5_CH_PER_GROUP, S5_STATE), (2 * S5_STATE) ** -0.5)
    d["s5_d"] = nrm((L, D_GROUP), 1.0)
    d["s5_w_glu"] = nrm((L, D_GROUP, D_GROUP), D_GROUP ** -0.5)
    d["s5_b_glu"] = nrm((L, D_GROUP), 0.01)
    d["cv_w"] = nrm((L, CONV_WIDTH, D_GROUP), CONV_WIDTH ** -0.5)
    d["cv_b"] = nrm((L, D_GROUP), 0.01)
    d["cv_gn_g"] = gain((L, D_GROUP))
    d["cv_gn_b"] = nrm((L, D_GROUP), 0.02)
    d["cv_w_pw"] = nrm((L, D_GROUP, D_GROUP), D_GROUP ** -0.5)
    d["cv_b_pw"] = nrm((L, D_GROUP), 0.01)
    d["lru_conv_w"] = nrm((L, LRU_CONV_WIDTH, D_GROUP), LRU_CONV_WIDTH ** -0.5)
    d["lru_conv_b"] = nrm((L, D_GROUP), 0.01)
    d["lru_w_r"] = nrm((L, LRU_HEADS, LRU_HEAD_DIM, LRU_HEAD_DIM), LRU_HEAD_DIM ** -0.5)
    d["lru_b_r"] = nrm((L, D_GROUP), 0.01)
    d["lru_w_i"] = nrm((L, LRU_HEADS, LRU_HEAD_DIM, LRU_HEAD_DIM), LRU_HEAD_DIM ** -0.5)
    d["lru_b_i"] = nrm((L, D_GROUP), 0.01)
    a_c = jax.random.uniform(next(keys), (L, D_GROUP), F32, 0.9, 0.999)
    a0 = a_c ** (1.0 / LRU_C)
    d["lru_lam"] = jnp.log(a0) - jnp.log1p(-a0)
    d["attn_w_kv"] = nrm((L, D_MODEL, 2 * D_GROUP), D_MODEL ** -0.5)
    d["w_out"] = nrm((L, D_MODEL, D_MODEL), D_MODEL ** -0.5 * DEEPNORM_BETA)
    d["b_out"] = nrm((L, D_MODEL), 0.01)
    d["ln1_g"] = gain((L, D_MODEL))
    d["ln1_b"] = nrm((L, D_MODEL), 0.02)
    d["ffn_w_up"] = nrm((L, D_MODEL, 2 * D_FF), D_MODEL ** -0.5)
    d["ffn_conv_w"] = nrm((L, FFN_CONV_WIDTH, 2 * D_FF), FFN_CONV_WIDTH ** -0.5)
    d["ffn_conv_b"] = nrm((L, 2 * D_FF), 0.01)
    d["ffn_w_down"] = nrm((L, D_FF, D_MODEL), D_FF ** -0.5 * DEEPNORM_BETA)
    d["ln2_g"] = gain((L, D_MODEL))
    d["ln2_b"] = nrm((L, D_MODEL), 0.02)
    return d


def reference(x, mem, ln_in_g, ln_in_b, w_in, b_in,
              s5_lam_re, s5_lam_im, s5_log_dt, s5_b_re, s5_b_im, s5_c_re, s5_c_im, s5_d, s5_w_glu, s5_b_glu,
              cv_w, cv_b, cv_gn_g, cv_gn_b, cv_w_pw, cv_b_pw,
              lru_conv_w, lru_conv_b, lru_w_r, lru_b_r, lru_w_i, lru_b_i, lru_lam,
              attn_w_kv, w_out, b_out, ln1_g, ln1_b,
              ffn_w_up, ffn_conv_w, ffn_conv_b, ffn_w_down, ln2_g, ln2_b):
    x = layer_norm(x, ln_in_g, ln_in_b)
    for l in range(DEPTH):
        h = x @ w_in[l] + b_in[l]
        s5_u, cv_v, cv_g, lru_g, lru_x, q = jnp.split(h, 6, axis=-1)
        y_s5 = s5_mixer(s5_u, s5_lam_re[l], s5_lam_im[l], s5_log_dt[l], s5_b_re[l], s5_b_im[l],
                        s5_c_re[l], s5_c_im[l], s5_d[l], s5_w_glu[l], s5_b_glu[l])
        y_cv = conformer_conv(cv_v, cv_g, cv_w[l], cv_b[l], cv_gn_g[l], cv_gn_b[l], cv_w_pw[l], cv_b_pw[l])
        y_lru = rglru_branch(lru_g, lru_x, lru_conv_w[l], lru_conv_b[l], lru_w_r[l], lru_b_r[l],
                             lru_w_i[l], lru_b_i[l], lru_lam[l])
        y_mem = memory_cross_attention(q, mem, attn_w_kv[l])
        mix = jnp.concatenate([y_s5, y_cv, y_lru, y_mem], axis=-1)
        mix = mix @ w_out[l].astype(F32) + b_out[l].astype(F32)
        x = layer_norm(DEEPNORM_ALPHA * x + mix, ln1_g[l], ln1_b[l])
        f = conv_ffn(x, ffn_w_up[l], ffn_conv_w[l], ffn_conv_b[l], ffn_w_down[l])
        x = layer_norm(DEEPNORM_ALPHA * x + f, ln2_g[l], ln2_b[l])
    return x
```

```python
import contextlib
import math
import numpy as np
import concourse.bass as bass
import concourse.mybir as mybir
from concourse.bass_utils import run_bass_kernel_spmd

F32 = mybir.dt.float32
BF16 = mybir.dt.bfloat16
I32 = mybir.dt.int32
AF = mybir.ActivationFunctionType
ALU = mybir.AluOpType

D = 1024
SEQ = 8192
DEPTH = 2
T = 512
SQ = 256
DFF = 2816
NPAIR = DFF // 128
ALPHA = (2 * DEPTH) ** 0.25
EPS = 1e-5
ENG_NAMES = ("pe", "dve", "act", "pool", "sp")
NSLOT = 3
RB_ENG = "pool"


class Sched:
    def __init__(self, nc, same_engine_sync=True):
        self.nc = nc
        self.ops = []
        self.per_eng = {e: [] for e in ENG_NAMES}
        self.last_w = {}
        self.readers = {}
        self.dma_cnt = {}
        self.same_engine_sync = same_engine_sync

    def _add(self, eng, fn, reads, writes, dsem=None):
        deps = []
        for k in reads:
            if k in self.last_w:
                deps.append(self.last_w[k] + ("raw",))
        for k in writes:
            if k in self.last_w:
                deps.append(self.last_w[k] + ("waw",))
            deps.extend(t + ("war",) for t in self.readers.get(k, ()))
        deps = [(d[0], d[1], self.dma_cnt[d[1]], d[3]) if d[0] == "d" else d for d in deps]
        idx = len(self.ops)
        if dsem is not None:
            c = self.dma_cnt.get(dsem, 0) + 1
            self.dma_cnt[dsem] = c
            tok = ("d", dsem, c)
        else:
            tok = ("e", eng, idx)
        op = dict(eng=eng, fn=fn, deps=deps, dsem=dsem, tok=tok, signal=False)
        self.ops.append(op)
        self.per_eng[eng].append(idx)
        for k in writes:
            self.last_w[k] = tok
            self.readers[k] = []
        for k in reads:
            if k not in writes:
                lst = self.readers.setdefault(k, [])
                lst[:] = [t for t in lst if not (t[0] == tok[0] and t[1] == tok[1])]
                lst.append(tok)
        return idx

    def op(self, eng, fn, reads=(), writes=()):
        return self._add(eng, fn, list(reads), list(writes))

    def dma(self, queue, out, in_, dsem, reads=(), writes=()):
        def fn(e, out=out, in_=in_):
            return e.dma_start(out=out, in_=in_)
        return self._add(queue, fn, list(reads), list(writes), dsem=dsem)

    def _skip_same(self, d, ename):
        if d[0] != "e" or d[1] != ename:
            return False
        return ename == "pe" or d[3] != "raw" or not self.same_engine_sync

    def emit(self, final_waits=()):
        nc = self.nc
        ops = self.ops
        for op in ops:
            for d in op["deps"]:
                if d[0] == "e" and not self._skip_same(d, op["eng"]):
                    ops[d[2]]["signal"] = True
        fin = []
        for eng, tok in final_waits:
            if tok[0] == "e":
                ops[tok[2]]["signal"] = True
            fin.append((eng, tok))
        cnt = {e: 0 for e in ENG_NAMES}
        for e in ENG_NAMES:
            for idx in self.per_eng[e]:
                if ops[idx]["signal"]:
                    cnt[e] += 1
                    ops[idx]["sval"] = cnt[e]
        self.sig_counts = cnt
        with contextlib.ExitStack() as st:
            esem = {e: st.enter_context(nc.semaphore("pg_" + e)) for e in ENG_NAMES}
            dsem = {n: st.enter_context(nc.semaphore("dm_%d" % i)) for i, n in enumerate(self.dma_cnt)}
            block = st.enter_context(nc.Block())

            def resolve(tok):
                if tok[0] == "e":
                    return ("e", tok[1]), esem[tok[1]], ops[tok[2]]["sval"]
                return ("d", tok[1]), dsem[tok[1]], 16 * tok[2]

            def run_engine(ename, e):
                waited = {}
                for idx in self.per_eng[ename]:
                    op = ops[idx]
                    need = {}
                    for d in op["deps"]:
                        if self._skip_same(d, ename):
                            continue
                        k, sem, val = resolve(d)
                        if waited.get(k, 0) >= val:
                            continue
                        if k not in need or need[k][1] < val:
                            need[k] = (sem, val)
                    items = list(need.items())
                    for k, (sem, val) in items[1:]:
                        e.wait_ge(sem, val)
                        waited[k] = val
                    ins = op["fn"](e)
                    if items:
                        k, (sem, val) = items[0]
                        ins._wait_ge(sem, val)
                        waited[k] = val
                    if op["dsem"] is not None:
                        ins.then_inc(dsem[op["dsem"]], 16)
                    elif op["signal"]:
                        ins.then_inc(esem[ename], 1)
                for eng2, tok in fin:
                    if eng2 == ename:
                        k, sem, val = resolve(tok)
                        e.wait_ge(sem, val)

            @block.tensor
            def _(e):
                run_engine("pe", e)

            @block.vector
            def _(e):
                run_engine("dve", e)

            @block.scalar
            def _(e):
                run_engine("act", e)

            @block.gpsimd
            def _(e):
                run_engine("pool", e)

            @block.sync
            def _(e):
                run_engine("sp", e)


def _cols(v):
    v = np.asarray(v, np.float32)
    return np.ascontiguousarray(v.reshape(-1, 128).T)


class _Pack:
    def __init__(self):
        self.parts = []
        self.off = {}
        self.n = 0

    def add(self, name, arr):
        arr = np.asarray(arr, np.float32)
        assert arr.shape[0] == 128
        self.off[name] = self.n
        self.parts.append(arr)
        self.n += arr.shape[1]

    def build(self):
        return np.ascontiguousarray(np.concatenate(self.parts, axis=1))


def _state_layout(a):
    a = np.asarray(a, np.float32).reshape(8, 2, 64)
    return np.ascontiguousarray(a.transpose(1, 2, 0).reshape(128, 8))


def _pvec(inp):
    P = _Pack()
    P.add("ln_in_g", _cols(inp["ln_in_g"]))
    P.add("ln_in_b", _cols(inp["ln_in_b"]))
    for l in range(DEPTH):
        s = "%d" % l
        P.add("b_in" + s, _cols(inp["b_in"][l]))
        for nm in ("s5_d", "s5_b_glu", "cv_b", "cv_gn_g", "cv_gn_b", "cv_b_pw", "lru_conv_b", "lru_b_r",
                   "lru_b_i", "lru_lam", "b_out", "ln1_g", "ln1_b", "ln2_g", "ln2_b", "ffn_conv_b"):
            P.add(nm + s, _cols(inp[nm][l]))
        P.add("cv_w" + s, np.concatenate([_cols(inp["cv_w"][l][k]) for k in range(31)], axis=1))
        P.add("lru_conv_w" + s, np.concatenate([_cols(inp["lru_conv_w"][l][k]) for k in range(4)], axis=1))
        P.add("ffn_conv_w" + s, np.concatenate([_cols(inp["ffn_conv_w"][l][k]) for k in range(3)], axis=1))
        P.add("s5_lre" + s, _state_layout(inp["s5_lam_re"][l]))
        P.add("s5_lim" + s, _state_layout(inp["s5_lam_im"][l]))
        P.add("s5_ldt" + s, _state_layout(np.broadcast_to(np.asarray(inp["s5_log_dt"][l])[:, None], (16, 64))))
    return P.build(), P.off


def _s5_layouts(inp, l):
    def rep(a):
        a = np.asarray(a, np.float32).reshape(8, 128)
        return np.broadcast_to(a.reshape(1, 1024), (128, 1024))
    frep = np.ascontiguousarray(np.stack([
        rep(inp["s5_lam_re"][l]), rep(inp["s5_lam_im"][l]),
        rep(np.broadcast_to(np.asarray(inp["s5_log_dt"][l])[:, None], (16, 64)))], axis=1))
    bt = np.zeros((2, 128, 8, 128), np.float32)
    ct = np.zeros((2, 128, 8, 128), np.float32)
    for ri, (bn, cn) in enumerate((("s5_b_re", "s5_c_re"), ("s5_b_im", "s5_c_im"))):
        B = np.asarray(inp[bn][l], np.float32)
        C = np.asarray(inp[cn][l], np.float32)
        for g in range(16):
            j, a = g // 2, g % 2
            r0 = (g % 8) * 16
            bt[ri, r0:r0 + 16, j, a * 64:(a + 1) * 64] = B[g].T
            ct[ri, a * 64:(a + 1) * 64, j, r0:r0 + 16] = C[g].T
    return frep, bt, ct


def _lru_bd(w):
    w = np.asarray(w, np.float32)
    o = np.zeros((128, 2, 128), np.float32)
    for h in range(4):
        ch, a = h // 2, h % 2
        o[a * 64:(a + 1) * 64, ch, a * 64:(a + 1) * 64] = w[h]
    return o


def build(NT, off, npv):
    S_TOK = NT * T
    nc = bass.Bass("TRN2", target_bir_lowering=False)

    def din(name, shape, dt=F32):
        return nc.dram_tensor(name, list(shape), dt, kind="ExternalInput").ap()

    xT = din("xT", [D, S_TOK])
    memT = din("memT", [D, 256])
    pvec_d = din("pvec", [128, npv])
    cst_d = din("cst", [128, 256 + SQ])
    cstb_d = din("cstb", [128, 384])
    w_in_d = [din("w_in%d" % l, [D, 1536]) for l in range(DEPTH)]
    w_out_d = [din("w_out%d" % l, [D, D]) for l in range(DEPTH)]
    w_up_d = [din("w_up%d" % l, [D, 2 * DFF]) for l in range(DEPTH)]
    w_dn_d = [din("w_dn%d" % l, [DFF, D]) for l in range(DEPTH)]
    w_kv_d = [din("w_kv%d" % l, [D, 512]) for l in range(DEPTH)]
    w_glu_d = [din("w_glu%d" % l, [256, 256]) for l in range(DEPTH)]
    w_pw_d = [din("w_pw%d" % l, [256, 256]) for l in range(DEPTH)]
    lru_r_d = [din("lru_r%d" % l, [128, 2, 128]) for l in range(DEPTH)]
    lru_i_d = [din("lru_i%d" % l, [128, 2, 128]) for l in range(DEPTH)]
    frep_d = [din("frep%d" % l, [128, 3, 1024]) for l in range(DEPTH)]
    bt_d = [din("bt%d" % l, [2, 128, 8, 128]) for l in range(DEPTH)]
    ct_d = [din("ct%d" % l, [2, 128, 8, 128]) for l in range(DEPTH)]
    outT = nc.dram_tensor("outT", [D, S_TOK], F32, kind="ExternalOutput").ap()

    st = contextlib.ExitStack()
    with st:
        def sb(name, shape, dt=F32):
            return st.enter_context(nc.sbuf_tensor(name, list(shape), dt))

        def psb(name):
            return st.enter_context(nc.psum_tensor(name, [128, T], F32))

        S = Sched(nc)
        xres = sb("xres", [128, 8, T])
        xbf = sb("xbf", [128, 8, T], BF16)
        NG = 10
        GT = [sb("g%d" % i, [128, 2, 544]) for i in range(NG)]
        mix = sb("mix", [128, 8, T], BF16)
        hff = sb("hff", [128, NPAIR, T], BF16)
        memT_bf = hff[:, 0:4, :].rearrange("p a (b m) -> p (a b) m", m=256)
        rcb = [hff[:, 2 * i:2 * i + 2, :].rearrange("p a b -> p (a b)").bitcast(F32) for i in range(2)]
        u_bf = sb("u_bf", [128, 2, T], BF16)
        q_bf = sb("q_bf", [128, 2, T], BF16)
        sre_bf2 = [sb("sre_bf%d" % i, [128, T], BF16) for i in range(2)]
        sim_bf2 = [sb("sim_bf%d" % i, [128, T], BF16) for i in range(2)]
        tiny = sb("tiny", [128, 8])
        xc_bf = sb("xc_bf", [128, 2, T], BF16)
        act_bf = sb("act_bf", [128, 2, T], BF16)
        yg_bf = sb("yg_bf", [128, 2, T], BF16)
        e_bf = [sb("e_bf%d" % i, [128, T], BF16) for i in range(2)]
        ring = [sb("ring%d" % i, [128, 4096], BF16) for i in range(NSLOT)]
        Bl = [[sb("Bl%d_%d" % (l, ri), [128, 8, 128], BF16) for ri in range(2)] for l in range(DEPTH)]
        Cl = [[sb("Cl%d_%d" % (l, ri), [128, 8, 128], BF16) for ri in range(2)] for l in range(DEPTH)]
        wglu = [sb("wglu%d" % l, [128, 2, 256], BF16) for l in range(DEPTH)]
        wpw = [sb("wpw%d" % l, [128, 2, 256], BF16) for l in range(DEPTH)]
        wlr = [sb("wlr%d" % l, [128, 2, 128], BF16) for l in range(DEPTH)]
        wli = [sb("wli%d" % l, [128, 2, 128], BF16) for l in range(DEPTH)]
        kT = [sb("kT%d" % l, [128, 2, 256], BF16) for l in range(DEPTH)]
        Vp = [sb("Vp%d" % l, [128, 4, 2, 128], BF16) for l in range(DEPTH)]
        tabC = [sb("tabC%d" % l, [128, 8, SQ]) for l in range(DEPTH)]
        tabS = [sb("tabS%d" % l, [128, 8, SQ]) for l in range(DEPTH)]
        pv = sb("pv", [128, npv])
        cst = sb("cst_sb", [128, 256 + SQ])
        cst_bf = sb("cst_bf", [128, 384], BF16)
        dg = sb("dg", [128, 4, 128], BF16)
        misc = sb("misc", [128, 64])
        stt_s5 = [sb("s5st%d" % l, [128, 16]) for l in range(DEPTH)]
        lru_st = [sb("lrust%d" % l, [128, 2]) for l in range(DEPTH)]
        cvtail = [sb("cvtail%d" % l, [128, 2, 30], BF16) for l in range(DEPTH)]
        xrtail = [sb("xrtail%d" % l, [128, 2, 3]) for l in range(DEPTH)]
        utail = [sb("utail%d" % l, [128, 2 * NPAIR, 2]) for l in range(DEPTH)]
        PS = [psb("ps%d" % i) for i in range(8)]

        def G(i, h):
            return GT[i][:, h, :], ("g", i, h)

        def pc(name, c=0, n=1):
            o = off[name] + c
            return pv[:, o:o + n]

        ps_rot = [0]

        def newps():
            i = ps_rot[0] % 6
            ps_rot[0] += 1
            return PS[i], ("ps", i)

        def ACT(out, in_, func, r, w, bias=None, scale=None):
            kw = {}
            if bias is not None:
                kw["bias"] = bias
            if scale is not None:
                kw["scale"] = scale
            S.op("act", lambda e: e.activation(out=out, in_=in_, func=func, **kw), r, w)

        def TT(eng, out, a, b, op, r, w):
            S.op(eng, lambda e: e.tensor_tensor(out=out, in0=a, in1=b, op=op), r, w)

        def TS(eng, out, a, s1, s2, op0, op1, r, w):
            if s2 is None:
                S.op(eng, lambda e: e.tensor_scalar(out=out, in0=a, scalar1=s1, scalar2=None, op0=op0), r, w)
            else:
                S.op(eng, lambda e: e.tensor_scalar(out=out, in0=a, scalar1=s1, scalar2=s2, op0=op0, op1=op1), r, w)

        def STT(out, in0, scalar, in1, op0, op1, r, w):
            S.op("dve", lambda e: e.scalar_tensor_tensor(out=out, in0=in0, scalar=scalar, in1=in1, op0=op0, op1=op1), r, w)

        def CP(eng, out, in_, r, w):
            if eng == "act":
                S.op("act", lambda e: e.copy(out=out, in_=in_), r, w)
            else:
                S.op(eng, lambda e: e.tensor_copy(out=out, in_=in_), r, w)

        def MM(out, lhsT, rhs, start, stop, r, w):
            S.op("pe", lambda e: e.matmul(out, lhsT=lhsT, rhs=rhs, start=start, stop=stop), r, w)

        def MS(eng, ap, val, w):
            S.op(eng, lambda e: e.memset(ap, val), (), w)

        ONES = cst[:, 0:128]
        GMAT = cst[:, 128:256]
        IOTA = cst[:, 256:256 + SQ]
        IDENT = cst_bf[:, 256:384]
        HO = [cst_bf[:, 0:128], cst_bf[:, 128:256]]
        M_EPS, M_ONE, M_HPI = 0, 1, 2

        def mc(l, k):
            o = 8 + l * 24 + k
            return misc[:, o:o + 1]

        S.dma("sp", pv[:], pvec_d, "su", writes=["pv"])
        S.dma("sp", cst[:], cst_d, "su", writes=["cst"])
        S.dma("pool", cst_bf[:], cstb_d, "suc", writes=["cst_bf"])
        MS("dve", misc[:, M_EPS:M_EPS + 1], EPS, ["misc"])
        MS("dve", misc[:, M_ONE:M_ONE + 1], 1.0, ["misc"])
        MS("dve", misc[:, M_HPI:M_HPI + 1], math.pi / 2, ["misc"])
        S.dma("pool", memT_bf, memT.rearrange("(kc p) m -> p kc m", p=128), "suc", writes=["memT"])
        for l in range(DEPTH):
            S.dma("pool", wglu[l][:], w_glu_d[l].rearrange("(kc p) n -> p kc n", p=128), "suc", writes=[("wglu", l)])
            S.dma("pool", wpw[l][:], w_pw_d[l].rearrange("(kc p) n -> p kc n", p=128), "suc", writes=[("wpw", l)])
            S.dma("pool", wlr[l][:], lru_r_d[l], "suc", writes=[("wlr", l)])
            S.dma("pool", wli[l][:], lru_i_d[l], "suc", writes=[("wli", l)])
            MS("dve", stt_s5[l][:], 0.0, [("s5st", l)])
            MS("dve", lru_st[l][:], 0.0, [("lrust", l)])
            MS("dve", cvtail[l][:], 0.0, [("cvtail", l)])
            MS("dve", xrtail[l][:], 0.0, [("xrtail", l)])
            MS("dve", utail[l][:], 0.0, [("utail", l, c) for c in range(2 * NPAIR)])
            MS("pool", Vp[l][:], 0.0, [("Vp", l)])

        TWO_PI = 2.0 * math.pi

        def range_reduce(ang, angk, tmpf, tmpi, keys):
            TS("dve", tmpf, ang, 1.0 / TWO_PI, None, ALU.mult, None, keys, keys)
            CP("dve", tmpi, tmpf, keys, keys)
            CP("dve", tmpf, tmpi, keys, keys)
            STT(ang, tmpf, -TWO_PI, ang, ALU.mult, ALU.add, keys, keys)

        for l in range(DEPTH):
            sl = "%d" % l
            tmp = misc[:, 4:6]
            ACT(tmp, pc("lru_lam" + sl, 0, 2), AF.Exp, ["pv", "misc"], ["misc"], scale=-1.0)
            ACT(tmp, tmp, AF.Ln, ["misc"], ["misc"], bias=misc[:, M_ONE:M_ONE + 1], scale=1.0)
            TS("dve", misc[:, 8 + l * 24:8 + l * 24 + 2], tmp, -8.0, None, ALU.mult, None, ["misc"], ["misc"])
            TS("dve", misc[:, 8 + l * 24 + 2:8 + l * 24 + 4], tmp, -16.0, None, ALU.mult, None, ["misc"], ["misc"])
            dtc = misc[:, 56:64]
            ACT(dtc, pc("s5_ldt" + sl, 0, 8), AF.Exp, ["pv", "misc"], ["misc"])
            rr = misc[:, 8 + l * 24 + 4:8 + l * 24 + 12]
            TT("dve", rr, pc("s5_lre" + sl, 0, 8), dtc, ALU.mult, ["pv", "misc"], ["misc"])
            ACT(rr, rr, AF.Exp, ["misc"], ["misc"])
            th = misc[:, 8 + l * 24 + 12:8 + l * 24 + 20]
            TT("dve", th, pc("s5_lim" + sl, 0, 8), dtc, ALU.mult, ["pv", "misc"], ["misc"])
            gk = [("g", i, h) for i in range(NG) for h in range(2)]
            angS = GT[0][:].rearrange("p a b -> p (a b)")
            angC = GT[1][:].rearrange("p a b -> p (a b)")
            tf = GT[2][:].rearrange("p a b -> p (a b)")
            ti = GT[3][:].rearrange("p a b -> p (a b)").bitcast(I32)
            HALF = 4 * SQ
            for half in range(2):
                for jj in range(4):
                    j = half * 4 + jj
                    TS("dve", angS[:, jj * SQ:(jj + 1) * SQ], IOTA[:, 0:SQ], th[:, j:j + 1], None, ALU.mult, None,
                       ["cst", "misc"] + gk, gk)
                TS("dve", angC[:, 0:HALF], angS[:, 0:HALF], math.pi / 2, None, ALU.add, None, gk, gk)
                range_reduce(angS[:, 0:HALF], None, tf[:, 0:HALF], ti[:, 0:HALF], gk)
                range_reduce(angC[:, 0:HALF], None, tf[:, 0:HALF], ti[:, 0:HALF], gk)
                ACT(tabS[l][:, half * 4:half * 4 + 4, :].rearrange("p a b -> p (a b)"), angS[:, 0:HALF], AF.Sin, gk, [("tab", l)])
                ACT(tabC[l][:, half * 4:half * 4 + 4, :].rearrange("p a b -> p (a b)"), angC[:, 0:HALF], AF.Sin, gk, [("tab", l)])
            def GF(i):
                return GT[i][:].rearrange("p a b -> p (a b)")[:, 0:1024]
            lre, lim, ldt = GF(0), GF(1), GF(2)
            S.dma("sp", lre, frep_d[l][:, 0, :], "su2", reads=gk, writes=gk)
            S.dma("sp", lim, frep_d[l][:, 1, :], "su2", reads=gk, writes=gk)
            S.dma("sp", ldt, frep_d[l][:, 2, :], "su2", reads=gk, writes=gk)
            ACT(ldt, ldt, AF.Exp, gk, gk)
            are, aim = GF(3), GF(4)
            TT("dve", are, lre, ldt, ALU.mult, gk, gk)
            TT("dve", aim, lim, ldt, ALU.mult, gk, gk)
            ACT(are, are, AF.Exp, gk, gk)
            sn, cs = GF(5), GF(6)
            CP("dve", sn, aim, gk, gk)
            TS("dve", cs, aim, math.pi / 2, None, ALU.add, None, gk, gk)
            tfa = GF(7)
            tia = GF(8).bitcast(I32)
            range_reduce(sn, None, tfa, tia, gk)
            range_reduce(cs, None, tfa, tia, gk)
            ACT(sn, sn, AF.Sin, gk, gk)
            ACT(cs, cs, AF.Sin, gk, gk)
            xx, yy = GF(7), GF(8)
            TT("dve", xx, are, cs, ALU.mult, gk, gk)
            TS("dve", xx, xx, -1.0, None, ALU.add, None, gk, gk)
            TT("dve", yy, are, sn, ALU.mult, gk, gk)
            den = GF(3)
            t9 = GF(9)
            TT("dve", den, lre, lre, ALU.mult, gk, gk)
            TT("dve", t9, lim, lim, ALU.mult, gk, gk)
            TT("dve", den, den, t9, ALU.add, gk, gk)
            S.op("dve", lambda e, den=den: e.reciprocal(out=den, in_=den), gk, gk)
            fre, fim = GF(4), GF(5)
            TT("dve", fre, xx, lre, ALU.mult, gk, gk)
            TT("dve", t9, yy, lim, ALU.mult, gk, gk)
            TT("dve", fre, fre, t9, ALU.add, gk, gk)
            TT("dve", fre, fre, den, ALU.mult, gk, gk)
            TT("dve", fim, yy, lre, ALU.mult, gk, gk)
            TT("dve", t9, xx, lim, ALU.mult, gk, gk)
            TT("dve", fim, fim, t9, ALU.subtract, gk, gk)
            TT("dve", fim, fim, den, ALU.mult, gk, gk)
            bre, bim = GF(0), GF(1)
            S.dma("sp", bre, bt_d[l][0].rearrange("p j q -> p (j q)"), "su2", reads=gk, writes=gk)
            S.dma("sp", bim, bt_d[l][1].rearrange("p j q -> p (j q)"), "su2", reads=gk, writes=gk)
            t6, t7 = GF(6), GF(7)
            TT("dve", t6, fre, bre, ALU.mult, gk, gk)
            TT("dve", t7, fim, bim, ALU.mult, gk, gk)
            TT("dve", Bl[l][0][:].rearrange("p j q -> p (j q)"), t6, t7, ALU.subtract, gk, [("Bl", l)])
            TT("dve", t6, fre, bim, ALU.mult, gk, gk)
            TT("dve", t7, fim, bre, ALU.mult, gk, gk)
            TT("dve", Bl[l][1][:].rearrange("p j q -> p (j q)"), t6, t7, ALU.add, gk, [("Bl", l)])
            S.dma("sp", bre, ct_d[l][0].rearrange("p j q -> p (j q)"), "su2", reads=gk, writes=gk)
            S.dma("sp", bim, ct_d[l][1].rearrange("p j q -> p (j q)"), "su2", reads=gk, writes=gk)
            CP("dve", Cl[l][0][:].rearrange("p j q -> p (j q)"), bre, gk, [("Cl", l)])
            TS("dve", Cl[l][1][:].rearrange("p j q -> p (j q)"), bim, -1.0, None, ALU.mult, None, gk, [("Cl", l)])
            slot = ring[0]
            S.dma("pool", slot[:].rearrange("p (kc n) -> p kc n", kc=8), w_kv_d[l].rearrange("(kc p) n -> p kc n", p=128),
                  "ring0", reads=[("ring", 0)], writes=[("ring", 0)])
            sv = slot[:].rearrange("p (kc n) -> p kc n", kc=8)
            for oc in range(2):
                pt, pk = newps()
                for kc in range(8):
                    MM(pt[:, 0:256], sv[:, kc, oc * 128:(oc + 1) * 128], memT_bf[:, kc, :], kc == 0, kc == 7,
                       [("ring", 0), "memT"], [pk])
                CP("act", kT[l][:, oc, :], pt[:, 0:256], [pk], [("kT", l)])
            for mh in range(2):
                pt, pk = newps()
                for kc in range(8):
                    MM(pt[:, 0:256], memT_bf[:, kc, mh * 128:(mh + 1) * 128], sv[:, kc, 256:512], kc == 0, kc == 7,
                       [("ring", 0), "memT"], [pk])
                for h in range(4):
                    a = h % 2
                    CP("act", Vp[l][:, h, mh, a * 64:(a + 1) * 64], pt[:, h * 64:(h + 1) * 64], [pk], [("Vp", l)])

        blocks = []
        for l in range(DEPTH):
            blocks += [(l, "in", i) for i in range(3)]
            blocks += [(l, "out", i) for i in range(2)]
            blocks += [(l, "up", i) for i in range(NPAIR // 2)]
            blocks += [(l, "dn", i) for i in range(6)]
        NB = len(blocks)
        wstate = dict(next_load=0)
        total_blocks = NB * NT

        wsc = nc.dram_tensor("wsc", [NB, 128, 4096], BF16, kind="Internal").ap()
        for b in range(NB):
            l, kind, i = blocks[b]
            key = ("wsc", b)
            ds = "pc%d" % b
            if kind == "in":
                S.dma("pool", wsc[b].rearrange("p (kc n) -> p kc n", kc=8),
                      w_in_d[l].rearrange("(kc p) n -> p kc n", p=128)[:, :, i * 512:(i + 1) * 512], ds, writes=[key])
            elif kind == "out":
                S.dma("pool", wsc[b].rearrange("p (kc n) -> p kc n", kc=8),
                      w_out_d[l].rearrange("(kc p) n -> p kc n", p=128)[:, :, i * 512:(i + 1) * 512], ds, writes=[key])
            elif kind == "up":
                src = w_up_d[l].rearrange("(kc p) n -> p kc n", p=128)
                dst = wsc[b].rearrange("p (kc n) -> p kc n", kc=8)
                S.dma("pool", dst[:, :, 0:256], src[:, :, i * 256:(i + 1) * 256], ds, writes=[key])
                S.dma("pool", dst[:, :, 256:512], src[:, :, DFF + i * 256:DFF + (i + 1) * 256], ds, writes=[key])
            else:
                nk = 4 if i < 5 else 2
                S.dma("pool", wsc[b][:, 0:nk * 1024].rearrange("p (kc n) -> p kc n", kc=nk),
                      w_dn_d[l].rearrange("(kc p) n -> p kc n", p=128)[:, i * 4:i * 4 + nk, :], ds, writes=[key])

        def issue_load(n):
            b = n % NB
            l, kind, i = blocks[b]
            s = n % NSLOT
            ln = 4096
            if kind == "dn" and i == 5:
                ln = 2048
            S.dma("sp", ring[s][:, 0:ln], wsc[b][:, 0:ln], "ring%d" % s, reads=[("wsc", b)], writes=[("ring", s)])

        def get_block(n):
            while wstate["next_load"] < min(n + NSLOT, total_blocks):
                issue_load(wstate["next_load"])
                wstate["next_load"] += 1
            return ring[n % NSLOT], ("ring", n % NSLOT)

        bctr = [0]

        def next_block(expect):
            n = bctr[0]
            assert blocks[n % NB][1] == expect, (blocks[n % NB], expect)
            bctr[0] += 1
            return get_block(n)

        def layer_norm(gname, bname, final=False):
            pm, pmk = PS[6], ("ps", 6)
            pe2, pe2k = PS[7], ("ps", 7)
            for c in range(8):
                MM(pm[:], ONES, xres[:, c, :], c == 0, c == 7, [("xres", c), "cst"], [pmk])
            for c in range(8):
                sq, sqk = G(8 + (c % 2), 0)
                ACT(sq[:, 0:T], xres[:, c, :], AF.Square, [("xres", c)], [sqk])
                MM(pe2[:], ONES, sq[:, 0:T], c == 0, c == 7, [sqk, "cst"], [pe2k])
            mean, mk = G(8, 1)
            var, vk = G(9, 1)
            CP("dve", mean[:, 0:T], pm[:], [pmk], [mk])
            TT("dve", var[:, 0:T], mean[:, 0:T], mean[:, 0:T], ALU.mult, [mk], [vk])
            TT("dve", var[:, 0:T], pe2[:], var[:, 0:T], ALU.subtract, [pe2k, vk], [vk])
            ACT(var[:, 0:T], var[:, 0:T], AF.Ln, [vk, "misc"], [vk], bias=misc[:, M_EPS:M_EPS + 1], scale=1.0)
            ACT(var[:, 0:T], var[:, 0:T], AF.Exp, [vk], [vk], scale=-0.5)
            for hf in range(4):
                xk = [("xres", c) for c in range(2 * hf, 2 * hf + 2)]
                xs = xres[:, 2 * hf:2 * hf + 2, :]
                TT("dve", xs, xs, mean[:, None, 0:T].broadcast_to([128, 2, T]), ALU.subtract, xk + [mk], xk)
                TT("dve", xs, xs, var[:, None, 0:T].broadcast_to([128, 2, T]), ALU.mult, xk + [vk], xk)
            for c in range(8):
                if final:
                    ob, obk = G(c // 2, c % 2)
                    ACT(ob[:, 0:T], xres[:, c, :], AF.Identity, [("xres", c), "pv"], [obk],
                        bias=pc(bname, c), scale=pc(gname, c))
                    continue
                ACT(xres[:, c, :], xres[:, c, :], AF.Identity, [("xres", c), "pv"], [("xres", c)],
                    bias=pc(bname, c), scale=pc(gname, c))
                CP("dve", xbf[:, c, :], xres[:, c, :], [("xres", c)], [("xbf", c)])

        for it in range(NT):
            t0 = it * T
            for c in range(8):
                S.dma("sp", xres[:, c, :], xT[c * 128:(c + 1) * 128, t0:t0 + T], "xin", writes=[("xres", c)])
            layer_norm("ln_in_g", "ln_in_b")
            for l in range(DEPTH):
                sl = "%d" % l
                uf = [G(0, 0), G(0, 1)]
                hcv = [(GT[1][:, hh, :].bitcast(BF16), ("g", 1, hh)) for hh in range(2)]
                lg = [G(2, 0), G(2, 1)]
                xr = [G(3, 0), G(3, 1)]
                win = dict(sv=None, sk=None)
                side_rot = [0]

                def newps_side():
                    i = 2 + side_rot[0] % 2
                    side_rot[0] += 1
                    return PS[i], ("ps", i)

                def win_chunk(oc, side):
                    if oc % 4 == 0:
                        slot, sk_ = next_block("in")
                        win["sv"] = slot[:].rearrange("p (kc n) -> p kc n", kc=8)
                        win["sk"] = sk_
                    sv, sk = win["sv"], win["sk"]
                    pt, pk = newps_side() if side else newps()
                    co = (oc % 4) * 128
                    for kc in range(8):
                        MM(pt[:], sv[:, kc, co:co + 128], xbf[:, kc, :], kc == 0, kc == 7, [sk, ("xbf", kc)], [pk])
                    bcol = pc("b_in" + sl, oc)
                    ch = oc % 2
                    if oc < 2:
                        ACT(uf[ch][0][:, 0:T], pt[:], AF.Identity, [pk, "pv"], [uf[ch][1]], bias=bcol, scale=1.0)
                        ACT(u_bf[:, ch, :], pt[:], AF.Identity, [pk, "pv"], [("u_bf", ch)], bias=bcol, scale=1.0)
                    elif oc < 4:
                        vt, vk = G(4, ch)
                        ACT(vt[:, 0:T], pt[:], AF.Identity, [pk, "pv"], [vk], bias=bcol, scale=1.0)
                    elif oc < 6:
                        sg, sgk = G(9, ch)
                        vt, vk = G(4, ch)
                        ACT(sg[:, 0:T], pt[:], AF.Sigmoid, [pk, "pv"], [sgk], bias=bcol, scale=1.0)
                        CP("dve", hcv[ch][0][:, 0:30], cvtail[l][:, ch, :], [("cvtail", l)], [hcv[ch][1]])
                        TT("dve", hcv[ch][0][:, 30:30 + T], vt[:, 0:T], sg[:, 0:T], ALU.mult, [vk, sgk], [hcv[ch][1]])
                        CP("dve", cvtail[l][:, ch, :], hcv[ch][0][:, T:T + 30], [hcv[ch][1]], [("cvtail", l)])
                    elif oc < 8:
                        ACT(lg[ch][0][:, 0:T], pt[:], AF.Gelu_apprx_tanh, [pk, "pv"], [lg[ch][1]], bias=bcol, scale=1.0)
                    elif oc < 10:
                        CP("dve", xr[ch][0][:, 0:3], xrtail[l][:, ch, :], [("xrtail", l)], [xr[ch][1]])
                        ACT(xr[ch][0][:, 3:3 + T], pt[:], AF.Identity, [pk, "pv"], [xr[ch][1]], bias=bcol, scale=1.0)
                        CP("dve", xrtail[l][:, ch, :], xr[ch][0][:, T:T + 3], [xr[ch][1]], [("xrtail", l)])
                    else:
                        ACT(q_bf[:, ch, :], pt[:], AF.Identity, [pk, "pv"], [("q_bf", ch)], bias=bcol, scale=1.0)

                win_chunk(0, False)
                win_chunk(1, False)

                def attn_front(hp):
                    po, pok = PS[0], ("ps", 0)
                    pd, pdk = PS[1], ("ps", 1)
                    combos = [(a, mh) for a in range(2) for mh in range(2)]
                    for n, (a, mh) in enumerate(combos):
                        h = 2 * hp + a
                        pt, pk = PS[2 + n % 2], ("ps", 2 + n % 2)
                        MM(pt[:], kT[l][a * 64:(a + 1) * 64, hp, mh * 128:(mh + 1) * 128], q_bf[a * 64:(a + 1) * 64, hp, :],
                           True, True, [("kT", l), ("q_bf", hp)], [pk])
                        eb = e_bf[n % 2]
                        ebk = ("e_bf", n % 2)
                        ACT(eb[:], pt[:], AF.Exp, [pk], [ebk], scale=0.125)
                        MM(po[:], Vp[l][:, h, mh, :], eb[:], n == 0, n == 3, [("Vp", l), ebk], [pok])
                        MM(pd[:], HO[a], eb[:], n == 0, n == 3, ["cst_bf", ebk], [pdk])
                    rc, rck = rcb[hp], ("rcb", hp)
                    ACT(rc[:], pd[:], AF.Ln, [pdk], [rck])
                    ACT(rc[:], rc[:], AF.Exp, [rck], [rck], scale=-1.0)

                def attn_back(hp):
                    rc, rck = rcb[hp], ("rcb", hp)
                    TT("dve", mix[:, 6 + hp, :], PS[0][:], rc[:], ALU.mult, [("ps", 0), rck], [("mix", 6 + hp)])

                def s5_gen():
                    yps = [(PS[6], ("ps", 6)), (PS[7], ("ps", 7))]
                    t1, k1 = G(5, 0)
                    t2, k2 = G(5, 1)
                    t3, k3 = G(6, 0)
                    t4, k4 = G(6, 1)
                    wir, kwir = t1, k1
                    wii, kwii = t3, k3
                    wr, _ = G(7, 0)
                    wi, _ = G(7, 1)
                    b1, kb1 = G(8, 0)
                    b2, kb2 = G(8, 1)
                    stk = ("s5st", l)
                    tk = [("tab", l)]
                    pre, prek = PS[4], ("ps", 4)
                    pim, pimk = PS[5], ("ps", 5)

                    def emit_bu(jj):
                        MM(pre[:], Bl[l][0][:, jj, :], u_bf[:, jj // 4, :], True, True, [("Bl", l), ("u_bf", jj // 4)], [prek])
                        MM(pim[:], Bl[l][1][:, jj, :], u_bf[:, jj // 4, :], True, True, [("Bl", l), ("u_bf", jj // 4)], [pimk])

                    emit_bu(0)
                    for j in range(8):
                        Cj = tabC[l][:, j, :]
                        Sj = tabS[l][:, j, :]
                        for sub in range(T // SQ):
                            cs_ = slice(sub * SQ, (sub + 1) * SQ)
                            kwr, kwi = ("wr", sub), ("wi", sub)
                            TT("dve", t1[:, cs_], pre[:, cs_], Cj, ALU.mult, [prek] + tk, [k1])
                            TT("dve", t2[:, cs_], pim[:, cs_], Sj, ALU.mult, [pimk] + tk, [k2])
                            TT("dve", t3[:, cs_], pim[:, cs_], Cj, ALU.mult, [pimk] + tk, [k3])
                            TT("dve", t4[:, cs_], pre[:, cs_], Sj, ALU.mult, [prek] + tk, [k4])
                            TT("dve", wir[:, cs_], t1[:, cs_], t2[:, cs_], ALU.add, [k1, k2], [kwir])
                            TT("dve", wii[:, cs_], t3[:, cs_], t4[:, cs_], ALU.subtract, [k3, k4], [kwii])
                            if sub == T // SQ - 1 and j < 7:
                                emit_bu(j + 1)
                            yield
                            rb = mc(l, 4 + j).broadcast_to([128, SQ])
                            ini_r, ini_i = stt_s5[l][:, j:j + 1], stt_s5[l][:, 8 + j:9 + j]
                            S.op("dve", lambda e, o=wr[:, cs_], d0=rb, d1=wir[:, cs_], ini=ini_r:
                                 e.tensor_tensor_scan(out=o, data0=d0, data1=d1, initial=ini, op0=ALU.mult, op1=ALU.add),
                                 [kwir, stk, "misc"], [kwr])
                            S.op("dve", lambda e, o=wi[:, cs_], d0=rb, d1=wii[:, cs_], ini=ini_i:
                                 e.tensor_tensor_scan(out=o, data0=d0, data1=d1, initial=ini, op0=ALU.mult, op1=ALU.add),
                                 [kwii, stk, "misc"], [kwi])
                            e0 = (sub + 1) * SQ - 1
                            Cl_, Sl_ = tabC[l][:, j, SQ - 1:SQ], tabS[l][:, j, SQ - 1:SQ]
                            TS("dve", tiny[:, 0:1], wi[:, e0:e0 + 1], Sl_, None, ALU.mult, None, [kwi] + tk, ["tiny"])
                            TS("dve", tiny[:, 1:2], wr[:, e0:e0 + 1], Sl_, None, ALU.mult, None, [kwr] + tk, ["tiny"])
                            STT(stt_s5[l][:, j:j + 1], wr[:, e0:e0 + 1], Cl_, tiny[:, 0:1], ALU.mult, ALU.subtract,
                                [kwr, "tiny"] + tk, [stk])
                            STT(stt_s5[l][:, 8 + j:9 + j], wi[:, e0:e0 + 1], Cl_, tiny[:, 1:2], ALU.mult, ALU.add,
                                [kwi, "tiny"] + tk, [stk])
                            sre_b, sim_b = sre_bf2[j % 2], sim_bf2[j % 2]
                            TT(RB_ENG, b1[:, cs_], wr[:, cs_], Cj, ALU.mult, [kwr] + tk, [kb1])
                            TT(RB_ENG, b2[:, cs_], wi[:, cs_], Sj, ALU.mult, [kwi] + tk, [kb2])
                            TT(RB_ENG, sre_b[:, cs_], b1[:, cs_], b2[:, cs_], ALU.subtract, [kb1, kb2], [("sre_bf", j % 2)])
                            TT(RB_ENG, b1[:, cs_], wi[:, cs_], Cj, ALU.mult, [kwi] + tk, [kb1])
                            TT(RB_ENG, b2[:, cs_], wr[:, cs_], Sj, ALU.mult, [kwr] + tk, [kb2])
                            TT(RB_ENG, sim_b[:, cs_], b1[:, cs_], b2[:, cs_], ALU.add, [kb1, kb2], [("sim_bf", j % 2)])
                            yield
                        m = j // 4
                        MM(yps[m][0][:], Cl[l][0][:, j, :], sre_bf2[j % 2][:], j % 4 == 0, False, [("Cl", l), ("sre_bf", j % 2)], [yps[m][1]])
                        MM(yps[m][0][:], Cl[l][1][:, j, :], sim_bf2[j % 2][:], False, j % 4 == 3, [("Cl", l), ("sim_bf", j % 2)], [yps[m][1]])
                    ygf = [G(5, 1), G(6, 1)]
                    yts = [G(5, 0), G(6, 0)]
                    for m in range(2):
                        STT(yts[m][0][:, 0:T], uf[m][0][:, 0:T], pc("s5_d" + sl, m), yps[m][0][:], ALU.mult, ALU.add,
                            [uf[m][1], "pv", yps[m][1]], [yts[m][1]])
                        ACT(ygf[m][0][:, 0:T], yts[m][0][:, 0:T], AF.Gelu_apprx_tanh, [yts[m][1]], [ygf[m][1]])
                        CP("dve", yg_bf[:, m, :], ygf[m][0][:, 0:T], [ygf[m][1]], [("yg_bf", m)])
                    yield
                    for oc in range(2):
                        pt, pk = PS[4 + oc], ("ps", 4 + oc)
                        for kc in range(2):
                            MM(pt[:], wglu[l][:, kc, oc * 128:(oc + 1) * 128], yg_bf[:, kc, :], kc == 0, kc == 1,
                               [("wglu", l), ("yg_bf", kc)], [pk])
                        ACT(yts[oc][0][:, 0:T], pt[:], AF.Sigmoid, [pk, "pv"], [yts[oc][1]], bias=pc("s5_b_glu" + sl, oc), scale=1.0)
                        TT("dve", mix[:, oc, :], ygf[oc][0][:, 0:T], yts[oc][0][:, 0:T], ALU.mult, [ygf[oc][1], yts[oc][1]],
                           [("mix", oc)])

                def side_gen():
                    for oc in range(2, 12):
                        win_chunk(oc, True)
                        yield
                    attn_front(0)
                    yield
                    for ch in range(2):
                        if ch == 1:
                            attn_back(0)
                            yield
                            attn_front(1)
                            yield
                        acc, acck = G(4, 0)
                        sq, sqk = G(4, 1)
                        mean, mk = G(9, 0)
                        var, vk = G(9, 1)
                        hb, hk = hcv[ch]
                        pcv, pcvk = newps_side()
                        for k in range(31):
                            dk = ("dg", k % 4)
                            ACT(dg[:, k % 4, :], IDENT, AF.Identity, ["cst_bf", "pv"], [dk], scale=pc("cv_w" + sl, k * 2 + ch))
                            MM(pcv[:], dg[:, k % 4, :], hb[:, k:k + T], k == 0, k == 30, [dk, hk], [pcvk])
                            if k % 8 == 7:
                                yield
                        ACT(acc[:, 0:T], pcv[:], AF.Identity, [pcvk, "pv"], [acck], bias=pc("cv_b" + sl, ch), scale=1.0)
                        ACT(sq[:, 0:T], acc[:, 0:T], AF.Square, [acck], [sqk])
                        pm, pmk = newps_side()
                        pe2, pe2k = newps_side()
                        MM(pm[:], GMAT, acc[:, 0:T], True, True, ["cst", acck], [pmk])
                        MM(pe2[:], GMAT, sq[:, 0:T], True, True, ["cst", sqk], [pe2k])
                        yield
                        CP("dve", mean[:, 0:T], pm[:], [pmk], [mk])
                        TT("dve", var[:, 0:T], mean[:, 0:T], mean[:, 0:T], ALU.mult, [mk], [vk])
                        TT("dve", var[:, 0:T], pe2[:], var[:, 0:T], ALU.subtract, [pe2k, vk], [vk])
                        ACT(var[:, 0:T], var[:, 0:T], AF.Ln, [vk, "misc"], [vk], bias=misc[:, M_EPS:M_EPS + 1], scale=1.0)
                        ACT(var[:, 0:T], var[:, 0:T], AF.Exp, [vk], [vk], scale=-0.5)
                        TT("dve", acc[:, 0:T], acc[:, 0:T], mean[:, 0:T], ALU.subtract, [acck, mk], [acck])
                        yield
                        TT("dve", acc[:, 0:T], acc[:, 0:T], var[:, 0:T], ALU.mult, [acck, vk], [acck])
                        ACT(act_bf[:, ch, :], acc[:, 0:T], AF.Silu, [acck, "pv"], [("act_bf", ch)],
                            bias=pc("cv_gn_b" + sl, ch), scale=pc("cv_gn_g" + sl, ch))
                        yield
                    for oc in range(2):
                        pt, pk = newps_side()
                        for kc in range(2):
                            MM(pt[:], wpw[l][:, kc, oc * 128:(oc + 1) * 128], act_bf[:, kc, :], kc == 0, kc == 1,
                               [("wpw", l), ("act_bf", kc)], [pk])
                        ACT(mix[:, 2 + oc, :], pt[:], AF.Identity, [pk, "pv"], [("mix", 2 + oc)],
                            bias=pc("cv_b_pw" + sl, oc), scale=1.0)
                    for ch in range(2):
                        if ch == 1:
                            attn_back(1)
                            yield
                        xc, xck = G(4, 0)
                        rg, rgk = G(4, 1)
                        ig, igk = G(9, 0)
                        e2, e2k = G(9, 1)
                        xb, xk = xr[ch]
                        ACT(xc[:, 0:T], xb[:, 0:T], AF.Identity, [xk, "pv"], [xck],
                            bias=pc("lru_conv_b" + sl, ch), scale=pc("lru_conv_w" + sl, 0 * 2 + ch))
                        yield
                        for k in range(1, 4):
                            STT(xc[:, 0:T], xb[:, k:k + T], pc("lru_conv_w" + sl, k * 2 + ch), xc[:, 0:T], ALU.mult, ALU.add,
                                [xk, "pv", xck], [xck])
                        CP("act", xc_bf[:, ch, :], xc[:, 0:T], [xck], [("xc_bf", ch)])
                        pr, prk = newps_side()
                        pi_, pik = newps_side()
                        MM(pr[:], wlr[l][:, ch, :], xc_bf[:, ch, :], True, True, [("wlr", l), ("xc_bf", ch)], [prk])
                        MM(pi_[:], wli[l][:, ch, :], xc_bf[:, ch, :], True, True, [("wli", l), ("xc_bf", ch)], [pik])
                        ACT(rg[:, 0:T], pr[:], AF.Sigmoid, [prk, "pv"], [rgk], bias=pc("lru_b_r" + sl, ch), scale=1.0)
                        ACT(ig[:, 0:T], pi_[:], AF.Sigmoid, [pik, "pv"], [igk], bias=pc("lru_b_i" + sl, ch), scale=1.0)
                        ACT(e2[:, 0:T], rg[:, 0:T], AF.Exp, [rgk, "misc"], [e2k], scale=mc(l, 2 + ch))
                        ACT(e2[:, 0:T], e2[:, 0:T], AF.Ln, [e2k, "misc"], [e2k], bias=misc[:, M_ONE:M_ONE + 1], scale=-1.0)
                        ACT(e2[:, 0:T], e2[:, 0:T], AF.Exp, [e2k], [e2k], scale=0.5)
                        yield
                        TT("dve", ig[:, 0:T], ig[:, 0:T], xc[:, 0:T], ALU.mult, [igk, xck], [igk])
                        aa, aak = xc, xck
                        ACT(aa[:, 0:T], rg[:, 0:T], AF.Exp, [rgk, "misc"], [aak], scale=mc(l, ch))
                        TT("dve", ig[:, 0:T], ig[:, 0:T], e2[:, 0:T], ALU.mult, [igk, e2k], [igk])
                        yield
                        S.op("dve", lambda e, o=rg[:, 0:T], d0=aa[:, 0:T], d1=ig[:, 0:T], ini=lru_st[l][:, ch:ch + 1]:
                             e.tensor_tensor_scan(out=o, data0=d0, data1=d1, initial=ini, op0=ALU.mult, op1=ALU.add),
                             [aak, igk, ("lrust", l)], [rgk])
                        CP("dve", lru_st[l][:, ch:ch + 1], rg[:, T - 1:T], [rgk], [("lrust", l)])
                        TT("dve", mix[:, 4 + ch, :], rg[:, 0:T], lg[ch][0][:, 0:T], ALU.mult, [rgk, lg[ch][1]], [("mix", 4 + ch)])
                        yield

                gens = [s5_gen(), side_gen()]
                alive = [True, True]
                while any(alive):
                    for gi in range(2):
                        if alive[gi]:
                            try:
                                next(gens[gi])
                            except StopIteration:
                                alive[gi] = False

                for oc in range(8):
                    if oc % 4 == 0:
                        slot, sk = next_block("out")
                        sv = slot[:].rearrange("p (kc n) -> p kc n", kc=8)
                    pt, pk = newps()
                    co = (oc % 4) * 128
                    for kc in range(8):
                        MM(pt[:], sv[:, kc, co:co + 128], mix[:, kc, :], kc == 0, kc == 7, [sk, ("mix", kc)], [pk])
                    tb, tbk = G(5, oc % 2)
                    ACT(tb[:, 0:T], pt[:], AF.Identity, [pk, "pv"], [tbk], bias=pc("b_out" + sl, oc), scale=1.0)
                    STT(xres[:, oc, :], xres[:, oc, :], ALPHA, tb[:, 0:T], ALU.mult, ALU.add, [("xres", oc), tbk], [("xres", oc)])
                layer_norm("ln1_g" + sl, "ln1_b" + sl)

                def ffn_finish(j, par, res):
                    gg, ggk = G(par * 3 + 2, 0)
                    ACT(gg[:, 0:T], res[1][0][:, 0:T], AF.Gelu_apprx_tanh, [res[1][1]], [ggk])
                    TT("pool", hff[:, j, :], res[0][0][:, 0:T], gg[:, 0:T], ALU.mult, [res[0][1], ggk], [("hff", j)])
                pend = None
                for j in range(NPAIR):
                    if j % 2 == 0:
                        slot, sk = next_block("up")
                        sv = slot[:].rearrange("p (kc n) -> p kc n", kc=8)
                    s_ = j % 2
                    par = j % 3
                    res = []
                    pts = []
                    for vg in range(2):
                        pt, pk = newps()
                        co = vg * 256 + s_ * 128
                        for kc in range(8):
                            MM(pt[:], sv[:, kc, co:co + 128], xbf[:, kc, :], kc == 0, kc == 7, [sk, ("xbf", kc)], [pk])
                        pts.append((pt, pk))
                    for vg in range(2):
                        cidx = vg * NPAIR + j
                        pt, pk = pts[vg]
                        ub, ubk = G(par * 3 + vg, 0)
                        yv, yvk = G(par * 3 + vg, 1)
                        utk = ("utail", l, cidx)
                        CP("pool", ub[:, 0:2], utail[l][:, cidx, :], [utk], [ubk])
                        CP("act", ub[:, 2:2 + T], pt[:], [pk], [ubk])
                        ACT(yv[:, 0:T], pt[:], AF.Identity, [pk, "pv"], [yvk],
                            bias=pc("ffn_conv_b" + sl, cidx), scale=pc("ffn_conv_w" + sl, 2 * 2 * NPAIR + cidx))
                        CP("pool", utail[l][:, cidx, :], ub[:, T:T + 2], [ubk], [utk])
                        res.append((yv, yvk, ub, ubk, cidx))
                    for k in (1, 0):
                        for vg in range(2):
                            yv, yvk, ub, ubk, cidx = res[vg]
                            STT(yv[:, 0:T], ub[:, k:k + T], pc("ffn_conv_w" + sl, k * 2 * NPAIR + cidx), yv[:, 0:T],
                                ALU.mult, ALU.add, [ubk, "pv", yvk], [yvk])
                    if pend is not None:
                        ffn_finish(*pend)
                    pend = (j, par, res)
                ffn_finish(*pend)
                dps = [(PS[i], ("ps", i)) for i in range(8)]
                for bi in range(6):
                    slot, sk = next_block("dn")
                    nk = 4 if bi < 5 else 2
                    sv = slot[:, 0:nk * 1024].rearrange("p (kc n) -> p kc n", kc=nk)
                    for oc in range(8):
                        for kk in range(nk):
                            kc = bi * 4 + kk
                            MM(dps[oc][0][:], sv[:, kk, oc * 128:(oc + 1) * 128], hff[:, kc, :], kc == 0, kc == NPAIR - 1,
                               [sk, ("hff", kc)], [dps[oc][1]])
                for oc in range(8):
                    STT(xres[:, oc, :], xres[:, oc, :], ALPHA, dps[oc][0][:], ALU.mult, ALU.add,
                        [("xres", oc), dps[oc][1]], [("xres", oc)])
                layer_norm("ln2_g" + sl, "ln2_b" + sl, final=(l == DEPTH - 1))
            for c in range(8):
                ob, obk = G(c // 2, c % 2)
                S.dma("sp", outT[c * 128:(c + 1) * 128, t0:t0 + T], ob[:, 0:T], "xout", reads=[obk])
        S.emit(final_waits=[("sp", ("d", "xout", S.dma_cnt["xout"]))])
    return nc


def _consts():
    c = np.zeros((128, 256 + SQ), np.float32)
    c[:, 0:128] = 1.0 / D
    for g in range(2):
        c[g * 64:(g + 1) * 64, 128 + g * 64:128 + (g + 1) * 64] = 1.0 / 64
    c[:, 256:256 + SQ] = np.arange(1, SQ + 1, dtype=np.float32)[None, :]
    cb = np.zeros((128, 384), np.float32)
    cb[:, 0:64] = 1.0
    cb[:, 128 + 64:256] = 1.0
    cb[:, 256:384] = np.eye(128, dtype=np.float32)
    return c, cb


def prepare(inputs, NT):
    inp = {k: np.asarray(v) for k, v in inputs.items()}
    pvec, off = _pvec(inp)
    c_, cb_ = _consts()
    common = {"pvec": pvec, "cst": c_, "cstb": cb_}
    for l in range(DEPTH):
        s = "%d" % l
        common["w_in" + s] = np.ascontiguousarray(inp["w_in"][l], np.float32)
        common["w_out" + s] = np.ascontiguousarray(inp["w_out"][l], np.float32)
        common["w_up" + s] = np.ascontiguousarray(inp["ffn_w_up"][l], np.float32)
        common["w_dn" + s] = np.ascontiguousarray(inp["ffn_w_down"][l], np.float32)
        common["w_kv" + s] = np.ascontiguousarray(inp["attn_w_kv"][l], np.float32)
        common["w_glu" + s] = np.ascontiguousarray(inp["s5_w_glu"][l], np.float32)
        common["w_pw" + s] = np.ascontiguousarray(inp["cv_w_pw"][l], np.float32)
        common["lru_r" + s] = _lru_bd(inp["lru_w_r"][l])
        common["lru_i" + s] = _lru_bd(inp["lru_w_i"][l])
        frep, bt, ct = _s5_layouts(inp, l)
        common["frep" + s] = frep
        common["bt" + s] = bt
        common["ct" + s] = ct
    maps = []
    for b in range(inp["x"].shape[0]):
        m = dict(common)
        m["xT"] = np.ascontiguousarray(inp["x"][b, :NT * T, :].T, np.float32)
        m["memT"] = np.ascontiguousarray(inp["mem"][b].T, np.float32)
        maps.append(m)
    return maps, off, pvec.shape[1]


def run(inputs, NT):
    maps, off, npv = prepare(inputs, NT)
    nc = build(NT, off, npv)
    res = run_bass_kernel_spmd(nc, maps, core_ids=list(range(len(maps))))
    out = np.stack([np.ascontiguousarray(r["outT"].T) for r in res.results], axis=0)
    return out.astype(np.float32)


def kernel(**inputs):
    return run(inputs, SEQ // T)
```

```python
import contextlib
import math
import numpy as np
import concourse.bass as bass
import concourse.mybir as mybir
from concourse.bass_utils import run_bass_kernel_spmd

F32 = mybir.dt.float32
BF16 = mybir.dt.bfloat16
I32 = mybir.dt.int32
AF = mybir.ActivationFunctionType
ALU = mybir.AluOpType

D = 1024
SEQ = 8192
DEPTH = 2
T = 512
SQ = 256
DFF = 2816
NPAIR = DFF // 128
ALPHA = (2 * DEPTH) ** 0.25
EPS = 1e-5
ENG_NAMES = ("pe", "dve", "act", "pool", "sp")
NSLOT = 3
RB_ENG = "pool"


class Sched:
    def __init__(self, nc, same_engine_sync=True):
        self.nc = nc
        self.ops = []
        self.per_eng = {e: [] for e in ENG_NAMES}
        self.last_w = {}
        self.readers = {}
        self.dma_cnt = {}
        self.same_engine_sync = same_engine_sync

    def _add(self, eng, fn, reads, writes, dsem=None):
        deps = []
        for k in reads:
            if k in self.last_w:
                deps.append(self.last_w[k] + ("raw",))
        for k in writes:
            if k in self.last_w:
                deps.append(self.last_w[k] + ("waw",))
            deps.extend(t + ("war",) for t in self.readers.get(k, ()))
        deps = [(d[0], d[1], self.dma_cnt[d[1]], d[3]) if d[0] == "d" else d for d in deps]
        idx = len(self.ops)
        if dsem is not None:
            c = self.dma_cnt.get(dsem, 0) + 1
            self.dma_cnt[dsem] = c
            tok = ("d", dsem, c)
        else:
            tok = ("e", eng, idx)
        op = dict(eng=eng, fn=fn, deps=deps, dsem=dsem, tok=tok, signal=False)
        self.ops.append(op)
        self.per_eng[eng].append(idx)
        for k in writes:
            self.last_w[k] = tok
            self.readers[k] = []
        for k in reads:
            if k not in writes:
                lst = self.readers.setdefault(k, [])
                lst[:] = [t for t in lst if not (t[0] == tok[0] and t[1] == tok[1])]
                lst.append(tok)
        return idx

    def op(self, eng, fn, reads=(), writes=()):
        return self._add(eng, fn, list(reads), list(writes))

    def dma(self, queue, out, in_, dsem, reads=(), writes=()):
        def fn(e, out=out, in_=in_):
            return e.dma_start(out=out, in_=in_)
        return self._add(queue, fn, list(reads), list(writes), dsem=dsem)

    def _skip_same(self, d, ename):
        if d[0] != "e" or d[1] != ename:
            return False
        return ename == "pe" or d[3] != "raw" or not self.same_engine_sync

    def emit(self, final_waits=()):
        nc = self.nc
        ops = self.ops
        for op in ops:
            for d in op["deps"]:
                if d[0] == "e" and not self._skip_same(d, op["eng"]):
                    ops[d[2]]["signal"] = True
        fin = []
        for eng, tok in final_waits:
            if tok[0] == "e":
                ops[tok[2]]["signal"] = True
            fin.append((eng, tok))
        cnt = {e: 0 for e in ENG_NAMES}
        for e in ENG_NAMES:
            for idx in self.per_eng[e]:
                if ops[idx]["signal"]:
                    cnt[e] += 1
                    ops[idx]["sval"] = cnt[e]
        self.sig_counts = cnt
        with contextlib.ExitStack() as st:
            esem = {e: st.enter_context(nc.semaphore("pg_" + e)) for e in ENG_NAMES}
            dsem = {n: st.enter_context(nc.semaphore("dm_%d" % i)) for i, n in enumerate(self.dma_cnt)}
            block = st.enter_context(nc.Block())

            def resolve(tok):
                if tok[0] == "e":
                    return ("e", tok[1]), esem[tok[1]], ops[tok[2]]["sval"]
                return ("d", tok[1]), dsem[tok[1]], 16 * tok[2]

            def run_engine(ename, e):
                waited = {}
                for idx in self.per_eng[ename]:
                    op = ops[idx]
                    need = {}
                    for d in op["deps"]:
                        if self._skip_same(d, ename):
                            continue
                        k, sem, val = resolve(d)
                        if waited.get(k, 0) >= val:
                            continue
                        if k not in need or need[k][1] < val:
                            need[k] = (sem, val)
                    items = list(need.items())
                    for k, (sem, val) in items[1:]:
                        e.wait_ge(sem, val)
                        waited[k] = val
                    ins = op["fn"](e)
                    if items:
                        k, (sem, val) = items[0]
                        ins._wait_ge(sem, val)
                        waited[k] = val
                    if op["dsem"] is not None:
                        ins.then_inc(dsem[op["dsem"]], 16)
                    elif op["signal"]:
                        ins.then_inc(esem[ename], 1)
                for eng2, tok in fin:
                    if eng2 == ename:
                        k, sem, val = resolve(tok)
                        e.wait_ge(sem, val)

            @block.tensor
            def _(e):
                run_engine("pe", e)

            @block.vector
            def _(e):
                run_engine("dve", e)

            @block.scalar
            def _(e):
                run_engine("act", e)

            @block.gpsimd
            def _(e):
                run_engine("pool", e)

            @block.sync
            def _(e):
                run_engine("sp", e)


def _cols(v):
    v = np.asarray(v, np.float32)
    return np.ascontiguousarray(v.reshape(-1, 128).T)


class _Pack:
    def __init__(self):
        self.parts = []
        self.off = {}
        self.n = 0

    def add(self, name, arr):
        arr = np.asarray(arr, np.float32)
        assert arr.shape[0] == 128
        self.off[name] = self.n
        self.parts.append(arr)
        self.n += arr.shape[1]

    def build(self):
        return np.ascontiguousarray(np.concatenate(self.parts, axis=1))


def _state_layout(a):
    a = np.asarray(a, np.float32).reshape(8, 2, 64)
    return np.ascontiguousarray(a.transpose(1, 2, 0).reshape(128, 8))


def _pvec(inp):
    P = _Pack()
    P.add("ln_in_g", _cols(inp["ln_in_g"]))
    P.add("ln_in_b", _cols(inp["ln_in_b"]))
    for l in range(DEPTH):
        s = "%d" % l
        P.add("b_in" + s, _cols(inp["b_in"][l]))
        for nm in ("s5_d", "s5_b_glu", "cv_b", "cv_gn_g", "cv_gn_b", "cv_b_pw", "lru_conv_b", "lru_b_r",
                   "lru_b_i", "lru_lam", "b_out", "ln1_g", "ln1_b", "ln2_g", "ln2_b", "ffn_conv_b"):
            P.add(nm + s, _cols(inp[nm][l]))
        P.add("cv_w" + s, np.concatenate([_cols(inp["cv_w"][l][k]) for k in range(31)], axis=1))
        P.add("lru_conv_w" + s, np.concatenate([_cols(inp["lru_conv_w"][l][k]) for k in range(4)], axis=1))
        P.add("ffn_conv_w" + s, np.concatenate([_cols(inp["ffn_conv_w"][l][k]) for k in range(3)], axis=1))
        P.add("s5_lre" + s, _state_layout(inp["s5_lam_re"][l]))
        P.add("s5_lim" + s, _state_layout(inp["s5_lam_im"][l]))
        P.add("s5_ldt" + s, _state_layout(np.broadcast_to(np.asarray(inp["s5_log_dt"][l])[:, None], (16, 64))))
    return P.build(), P.off


def _s5_layouts(inp, l):
    def rep(a):
        a = np.asarray(a, np.float32).reshape(8, 128)
        return np.broadcast_to(a.reshape(1, 1024), (128, 1024))
    frep = np.ascontiguousarray(np.stack([
        rep(inp["s5_lam_re"][l]), rep(inp["s5_lam_im"][l]),
        rep(np.broadcast_to(np.asarray(inp["s5_log_dt"][l])[:, None], (16, 64)))], axis=1))
    bt = np.zeros((2, 128, 8, 128), np.float32)
    ct = np.zeros((2, 128, 8, 128), np.float32)
    for ri, (bn, cn) in enumerate((("s5_b_re", "s5_c_re"), ("s5_b_im", "s5_c_im"))):
        B = np.asarray(inp[bn][l], np.float32)
        C = np.asarray(inp[cn][l], np.float32)
        for g in range(16):
            j, a = g // 2, g % 2
            r0 = (g % 8) * 16
            bt[ri, r0:r0 + 16, j, a * 64:(a + 1) * 64] = B[g].T
            ct[ri, a * 64:(a + 1) * 64, j, r0:r0 + 16] = C[g].T
    return frep, bt, ct


def _lru_bd(w):
    w = np.asarray(w, np.float32)
    o = np.zeros((128, 2, 128), np.float32)
    for h in range(4):
        ch, a = h // 2, h % 2
        o[a * 64:(a + 1) * 64, ch, a * 64:(a + 1) * 64] = w[h]
    return o


def build(NT, off, npv):
    S_TOK = NT * T
    nc = bass.Bass("TRN2", target_bir_lowering=False)

    def din(name, shape, dt=F32):
        return nc.dram_tensor(name, list(shape), dt, kind="ExternalInput").ap()

    xT = din("xT", [D, S_TOK])
    memT = din("memT", [D, 256])
    pvec_d = din("pvec", [128, npv])
    cst_d = din("cst", [128, 256 + SQ])
    cstb_d = din("cstb", [128, 384])
    w_in_d = [din("w_in%d" % l, [D, 1536]) for l in range(DEPTH)]
    w_out_d = [din("w_out%d" % l, [D, D]) for l in range(DEPTH)]
    w_up_d = [din("w_up%d" % l, [D, 2 * DFF]) for l in range(DEPTH)]
    w_dn_d = [din("w_dn%d" % l, [DFF, D]) for l in range(DEPTH)]
    w_kv_d = [din("w_kv%d" % l, [D, 512]) for l in range(DEPTH)]
    w_glu_d = [din("w_glu%d" % l, [256, 256]) for l in range(DEPTH)]
    w_pw_d = [din("w_pw%d" % l, [256, 256]) for l in range(DEPTH)]
    lru_r_d = [din("lru_r%d" % l, [128, 2, 128]) for l in range(DEPTH)]
    lru_i_d = [din("lru_i%d" % l, [128, 2, 128]) for l in range(DEPTH)]
    frep_d = [din("frep%d" % l, [128, 3, 1024]) for l in range(DEPTH)]
    bt_d = [din("bt%d" % l, [2, 128, 8, 128]) for l in range(DEPTH)]
    ct_d = [din("ct%d" % l, [2, 128, 8, 128]) for l in range(DEPTH)]
    outT = nc.dram_tensor("outT", [D, S_TOK], F32, kind="ExternalOutput").ap()

    st = contextlib.ExitStack()
    with st:
        def sb(name, shape, dt=F32):
            return st.enter_context(nc.sbuf_tensor(name, list(shape), dt))

        def psb(name):
            return st.enter_context(nc.psum_tensor(name, [128, T], F32))

        S = Sched(nc)
        xres = sb("xres", [128, 8, T])
        xbf = sb("xbf", [128, 8, T], BF16)
        NG = 10
        GT = [sb("g%d" % i, [128, 2, 544]) for i in range(NG)]
        mix = sb("mix", [128, 8, T], BF16)
        hff = sb("hff", [128, NPAIR, T], BF16)
        memT_bf = hff[:, 0:4, :].rearrange("p a (b m) -> p (a b) m", m=256)
        rcb = [hff[:, 2 * i:2 * i + 2, :].rearrange("p a b -> p (a b)").bitcast(F32) for i in range(2)]
        u_bf = sb("u_bf", [128, 2, T], BF16)
        q_bf = sb("q_bf", [128, 2, T], BF16)
        sre_bf2 = [sb("sre_bf%d" % i, [128, T], BF16) for i in range(2)]
        sim_bf2 = [sb("sim_bf%d" % i, [128, T], BF16) for i in range(2)]
        tiny = sb("tiny", [128, 8])
        xc_bf = sb("xc_bf", [128, 2, T], BF16)
        act_bf = sb("act_bf", [128, 2, T], BF16)
        yg_bf = sb("yg_bf", [128, 2, T], BF16)
        e_bf = [sb("e_bf%d" % i, [128, T], BF16) for i in range(2)]
        ring = [sb("ring%d" % i, [128, 4096], BF16) for i in range(NSLOT)]
        Bl = [[sb("Bl%d_%d" % (l, ri), [128, 8, 128], BF16) for ri in range(2)] for l in range(DEPTH)]
        Cl = [[sb("Cl%d_%d" % (l, ri), [128, 8, 128], BF16) for ri in range(2)] for l in range(DEPTH)]
        wglu = [sb("wglu%d" % l, [128, 2, 256], BF16) for l in range(DEPTH)]
        wpw = [sb("wpw%d" % l, [128, 2, 256], BF16) for l in range(DEPTH)]
        wlr = [sb("wlr%d" % l, [128, 2, 128], BF16) for l in range(DEPTH)]
        wli = [sb("wli%d" % l, [128, 2, 128], BF16) for l in range(DEPTH)]
        kT = [sb("kT%d" % l, [128, 2, 256], BF16) for l in range(DEPTH)]
        Vp = [sb("Vp%d" % l, [128, 4, 2, 128], BF16) for l in range(DEPTH)]
        tabC = [sb("tabC%d" % l, [128, 8, SQ]) for l in range(DEPTH)]
        tabS = [sb("tabS%d" % l, [128, 8, SQ]) for l in range(DEPTH)]
        pv = sb("pv", [128, npv])
        cst = sb("cst_sb", [128, 256 + SQ])
        cst_bf = sb("cst_bf", [128, 384], BF16)
        dg = sb("dg", [128, 4, 128], BF16)
        misc = sb("misc", [128, 64])
        stt_s5 = [sb("s5st%d" % l, [128, 16]) for l in range(DEPTH)]
        lru_st = [sb("lrust%d" % l, [128, 2]) for l in range(DEPTH)]
        cvtail = [sb("cvtail%d" % l, [128, 2, 30], BF16) for l in range(DEPTH)]
        xrtail = [sb("xrtail%d" % l, [128, 2, 3]) for l in range(DEPTH)]
        utail = [sb("utail%d" % l, [128, 2 * NPAIR, 2]) for l in range(DEPTH)]
        PS = [psb("ps%d" % i) for i in range(8)]

        def G(i, h):
            return GT[i][:, h, :], ("g", i, h)

        def pc(name, c=0, n=1):
            o = off[name] + c
            return pv[:, o:o + n]

        ps_rot = [0]

        def newps():
            i = ps_rot[0] % 6
            ps_rot[0] += 1
            return PS[i], ("ps", i)

        def ACT(out, in_, func, r, w, bias=None, scale=None):
            kw = {}
            if bias is not None:
                kw["bias"] = bias
            if scale is not None:
                kw["scale"] = scale
            S.op("act", lambda e: e.activation(out=out, in_=in_, func=func, **kw), r, w)

        def TT(eng, out, a, b, op, r, w):
            S.op(eng, lambda e: e.tensor_tensor(out=out, in0=a, in1=b, op=op), r, w)

        def TS(eng, out, a, s1, s2, op0, op1, r, w):
            if s2 is None:
                S.op(eng, lambda e: e.tensor_scalar(out=out, in0=a, scalar1=s1, scalar2=None, op0=op0), r, w)
            else:
                S.op(eng, lambda e: e.tensor_scalar(out=out, in0=a, scalar1=s1, scalar2=s2, op0=op0, op1=op1), r, w)

        def STT(out, in0, scalar, in1, op0, op1, r, w):
            S.op("dve", lambda e: e.scalar_tensor_tensor(out=out, in0=in0, scalar=scalar, in1=in1, op0=op0, op1=op1), r, w)

        def CP(eng, out, in_, r, w):
            if eng == "act":
                S.op("act", lambda e: e.copy(out=out, in_=in_), r, w)
            else:
                S.op(eng, lambda e: e.tensor_copy(out=out, in_=in_), r, w)

        def MM(out, lhsT, rhs, start, stop, r, w):
            S.op("pe", lambda e: e.matmul(out, lhsT=lhsT, rhs=rhs, start=start, stop=stop), r, w)

        def MS(eng, ap, val, w):
            S.op(eng, lambda e: e.memset(ap, val), (), w)

        ONES = cst[:, 0:128]
        GMAT = cst[:, 128:256]
        IOTA = cst[:, 256:256 + SQ]
        IDENT = cst_bf[:, 256:384]
        HO = [cst_bf[:, 0:128], cst_bf[:, 128:256]]
        M_EPS, M_ONE, M_HPI = 0, 1, 2

        def mc(l, k):
            o = 8 + l * 24 + k
            return misc[:, o:o + 1]

        S.dma("sp", pv[:], pvec_d, "su", writes=["pv"])
        S.dma("sp", cst[:], cst_d, "su", writes=["cst"])
        S.dma("pool", cst_bf[:], cstb_d, "suc", writes=["cst_bf"])
        MS("dve", misc[:, M_EPS:M_EPS + 1], EPS, ["misc"])
        MS("dve", misc[:, M_ONE:M_ONE + 1], 1.0, ["misc"])
        MS("dve", misc[:, M_HPI:M_HPI + 1], math.pi / 2, ["misc"])
        S.dma("pool", memT_bf, memT.rearrange("(kc p) m -> p kc m", p=128), "suc", writes=["memT"])
        for l in range(DEPTH):
            S.dma("pool", wglu[l][:], w_glu_d[l].rearrange("(kc p) n -> p kc n", p=128), "suc", writes=[("wglu", l)])
            S.dma("pool", wpw[l][:], w_pw_d[l].rearrange("(kc p) n -> p kc n", p=128), "suc", writes=[("wpw", l)])
            S.dma("pool", wlr[l][:], lru_r_d[l], "suc", writes=[("wlr", l)])
            S.dma("pool", wli[l][:], lru_i_d[l], "suc", writes=[("wli", l)])
            MS("dve", stt_s5[l][:], 0.0, [("s5st", l)])
            MS("dve", lru_st[l][:], 0.0, [("lrust", l)])
            MS("dve", cvtail[l][:], 0.0, [("cvtail", l)])
            MS("dve", xrtail[l][:], 0.0, [("xrtail", l)])
            MS("dve", utail[l][:], 0.0, [("utail", l, c) for c in range(2 * NPAIR)])
            MS("pool", Vp[l][:], 0.0, [("Vp", l)])

        TWO_PI = 2.0 * math.pi

        def range_reduce(ang, angk, tmpf, tmpi, keys):
            TS("dve", tmpf, ang, 1.0 / TWO_PI, None, ALU.mult, None, keys, keys)
            CP("dve", tmpi, tmpf, keys, keys)
            CP("dve", tmpf, tmpi, keys, keys)
            STT(ang, tmpf, -TWO_PI, ang, ALU.mult, ALU.add, keys, keys)

        for l in range(DEPTH):
            sl = "%d" % l
            tmp = misc[:, 4:6]
            ACT(tmp, pc("lru_lam" + sl, 0, 2), AF.Exp, ["pv", "misc"], ["misc"], scale=-1.0)
            ACT(tmp, tmp, AF.Ln, ["misc"], ["misc"], bias=misc[:, M_ONE:M_ONE + 1], scale=1.0)
            TS("dve", misc[:, 8 + l * 24:8 + l * 24 + 2], tmp, -8.0, None, ALU.mult, None, ["misc"], ["misc"])
            TS("dve", misc[:, 8 + l * 24 + 2:8 + l * 24 + 4], tmp, -16.0, None, ALU.mult, None, ["misc"], ["misc"])
            dtc = misc[:, 56:64]
            ACT(dtc, pc("s5_ldt" + sl, 0, 8), AF.Exp, ["pv", "misc"], ["misc"])
            rr = misc[:, 8 + l * 24 + 4:8 + l * 24 + 12]
            TT("dve", rr, pc("s5_lre" + sl, 0, 8), dtc, ALU.mult, ["pv", "misc"], ["misc"])
            ACT(rr, rr, AF.Exp, ["misc"], ["misc"])
            th = misc[:, 8 + l * 24 + 12:8 + l * 24 + 20]
            TT("dve", th, pc("s5_lim" + sl, 0, 8), dtc, ALU.mult, ["pv", "misc"], ["misc"])
            gk = [("g", i, h) for i in range(NG) for h in range(2)]
            angS = GT[0][:].rearrange("p a b -> p (a b)")
            angC = GT[1][:].rearrange("p a b -> p (a b)")
            tf = GT[2][:].rearrange("p a b -> p (a b)")
            ti = GT[3][:].rearrange("p a b -> p (a b)").bitcast(I32)
            HALF = 4 * SQ
            for half in range(2):
                for jj in range(4):
                    j = half * 4 + jj
                    TS("dve", angS[:, jj * SQ:(jj + 1) * SQ], IOTA[:, 0:SQ], th[:, j:j + 1], None, ALU.mult, None,
                       ["cst", "misc"] + gk, gk)
                TS("dve", angC[:, 0:HALF], angS[:, 0:HALF], math.pi / 2, None, ALU.add, None, gk, gk)
                range_reduce(angS[:, 0:HALF], None, tf[:, 0:HALF], ti[:, 0:HALF], gk)
                range_reduce(angC[:, 0:HALF], None, tf[:, 0:HALF], ti[:, 0:HALF], gk)
                ACT(tabS[l][:, half * 4:half * 4 + 4, :].rearrange("p a b -> p (a b)"), angS[:, 0:HALF], AF.Sin, gk, [("tab", l)])
                ACT(tabC[l][:, half * 4:half * 4 + 4, :].rearrange("p a b -> p (a b)"), angC[:, 0:HALF], AF.Sin, gk, [("tab", l)])
            def GF(i):
                return GT[i][:].rearrange("p a b -> p (a b)")[:, 0:1024]
            lre, lim, ldt = GF(0), GF(1), GF(2)
            S.dma("sp", lre, frep_d[l][:, 0, :], "su2", reads=gk, writes=gk)
            S.dma("sp", lim, frep_d[l][:, 1, :], "su2", reads=gk, writes=gk)
            S.dma("sp", ldt, frep_d[l][:, 2, :], "su2", reads=gk, writes=gk)
            ACT(ldt, ldt, AF.Exp, gk, gk)
            are, aim = GF(3), GF(4)
            TT("dve", are, lre, ldt, ALU.mult, gk, gk)
            TT("dve", aim, lim, ldt, ALU.mult, gk, gk)
            ACT(are, are, AF.Exp, gk, gk)
            sn, cs = GF(5), GF(6)
            CP("dve", sn, aim, gk, gk)
            TS("dve", cs, aim, math.pi / 2, None, ALU.add, None, gk, gk)
            tfa = GF(7)
            tia = GF(8).bitcast(I32)
            range_reduce(sn, None, tfa, tia, gk)
            range_reduce(cs, None, tfa, tia, gk)
            ACT(sn, sn, AF.Sin, gk, gk)
            ACT(cs, cs, AF.Sin, gk, gk)
            xx, yy = GF(7), GF(8)
            TT("dve", xx, are, cs, ALU.mult, gk, gk)
            TS("dve", xx, xx, -1.0, None, ALU.add, None, gk, gk)
            TT("dve", yy, are, sn, ALU.mult, gk, gk)
            den = GF(3)
            t9 = GF(9)
            TT("dve", den, lre, lre, ALU.mult, gk, gk)
            TT("dve", t9, lim, lim, ALU.mult, gk, gk)
            TT("dve", den, den, t9, ALU.add, gk, gk)
            S.op("dve", lambda e, den=den: e.reciprocal(out=den, in_=den), gk, gk)
            fre, fim = GF(4), GF(5)
            TT("dve", fre, xx, lre, ALU.mult, gk, gk)
            TT("dve", t9, yy, lim, ALU.mult, gk, gk)
            TT("dve", fre, fre, t9, ALU.add, gk, gk)
            TT("dve", fre, fre, den, ALU.mult, gk, gk)
            TT("dve", fim, yy, lre, ALU.mult, gk, gk)
            TT("dve", t9, xx, lim, ALU.mult, gk, gk)
            TT("dve", fim, fim, t9, ALU.subtract, gk, gk)
            TT("dve", fim, fim, den, ALU.mult, gk, gk)
            bre, bim = GF(0), GF(1)
            S.dma("sp", bre, bt_d[l][0].rearrange("p j q -> p (j q)"), "su2", reads=gk, writes=gk)
            S.dma("sp", bim, bt_d[l][1].rearrange("p j q -> p (j q)"), "su2", reads=gk, writes=gk)
            t6, t7 = GF(6), GF(7)
            TT("dve", t6, fre, bre, ALU.mult, gk, gk)
            TT("dve", t7, fim, bim, ALU.mult, gk, gk)
            TT("dve", Bl[l][0][:].rearrange("p j q -> p (j q)"), t6, t7, ALU.subtract, gk, [("Bl", l)])
            TT("dve", t6, fre, bim, ALU.mult, gk, gk)
            TT("dve", t7, fim, bre, ALU.mult, gk, gk)
            TT("dve", Bl[l][1][:].rearrange("p j q -> p (j q)"), t6, t7, ALU.add, gk, [("Bl", l)])
            S.dma("sp", bre, ct_d[l][0].rearrange("p j q -> p (j q)"), "su2", reads=gk, writes=gk)
            S.dma("sp", bim, ct_d[l][1].rearrange("p j q -> p (j q)"), "su2", reads=gk, writes=gk)
            CP("dve", Cl[l][0][:].rearrange("p j q -> p (j q)"), bre, gk, [("Cl", l)])
            TS("dve", Cl[l][1][:].rearrange("p j q -> p (j q)"), bim, -1.0, None, ALU.mult, None, gk, [("Cl", l)])
            slot = ring[0]
            S.dma("pool", slot[:].rearrange("p (kc n) -> p kc n", kc=8), w_kv_d[l].rearrange("(kc p) n -> p kc n", p=128),
                  "ring0", reads=[("ring", 0)], writes=[("ring", 0)])
            sv = slot[:].rearrange("p (kc n) -> p kc n", kc=8)
            for oc in range(2):
                pt, pk = newps()
                for kc in range(8):
                    MM(pt[:, 0:256], sv[:, kc, oc * 128:(oc + 1) * 128], memT_bf[:, kc, :], kc == 0, kc == 7,
                       [("ring", 0), "memT"], [pk])
                CP("act", kT[l][:, oc, :], pt[:, 0:256], [pk], [("kT", l)])
            for mh in range(2):
                pt, pk = newps()
                for kc in range(8):
                    MM(pt[:, 0:256], memT_bf[:, kc, mh * 128:(mh + 1) * 128], sv[:, kc, 256:512], kc == 0, kc == 7,
                       [("ring", 0), "memT"], [pk])
                for h in range(4):
                    a = h % 2
                    CP("act", Vp[l][:, h, mh, a * 64:(a + 1) * 64], pt[:, h * 64:(h + 1) * 64], [pk], [("Vp", l)])

        blocks = []
        for l in range(DEPTH):
            blocks += [(l, "in", i) for i in range(3)]
            blocks += [(l, "out", i) for i in range(2)]
            blocks += [(l, "up", i) for i in range(NPAIR // 2)]
            blocks += [(l, "dn", i) for i in range(6)]
        NB = len(blocks)
        wstate = dict(next_load=0)
        total_blocks = NB * NT

        wsc = nc.dram_tensor("wsc", [NB, 128, 4096], BF16, kind="Internal").ap()
        for b in range(NB):
            l, kind, i = blocks[b]
            key = ("wsc", b)
            ds = "pc%d" % b
            if kind == "in":
                S.dma("pool", wsc[b].rearrange("p (kc n) -> p kc n", kc=8),
                      w_in_d[l].rearrange("(kc p) n -> p kc n", p=128)[:, :, i * 512:(i + 1) * 512], ds, writes=[key])
            elif kind == "out":
                S.dma("pool", wsc[b].rearrange("p (kc n) -> p kc n", kc=8),
                      w_out_d[l].rearrange("(kc p) n -> p kc n", p=128)[:, :, i * 512:(i + 1) * 512], ds, writes=[key])
            elif kind == "up":
                src = w_up_d[l].rearrange("(kc p) n -> p kc n", p=128)
                dst = wsc[b].rearrange("p (kc n) -> p kc n", kc=8)
                S.dma("pool", dst[:, :, 0:256], src[:, :, i * 256:(i + 1) * 256], ds, writes=[key])
                S.dma("pool", dst[:, :, 256:512], src[:, :, DFF + i * 256:DFF + (i + 1) * 256], ds, writes=[key])
            else:
                nk = 4 if i < 5 else 2
                S.dma("pool", wsc[b][:, 0:nk * 1024].rearrange("p (kc n) -> p kc n", kc=nk),
                      w_dn_d[l].rearrange("(kc p) n -> p kc n", p=128)[:, i * 4:i * 4 + nk, :], ds, writes=[key])

        def issue_load(n):
            b = n % NB
            l, kind, i = blocks[b]
            s = n % NSLOT
            ln = 4096
            if kind == "dn" and i == 5:
                ln = 2048
            S.dma("sp", ring[s][:, 0:ln], wsc[b][:, 0:ln], "ring%d" % s, reads=[("wsc", b)], writes=[("ring", s)])

        def get_block(n):
            while wstate["next_load"] < min(n + NSLOT, total_blocks):
                issue_load(wstate["next_load"])
                wstate["next_load"] += 1
            return ring[n % NSLOT], ("ring", n % NSLOT)

        bctr = [0]

        def next_block(expect):
            n = bctr[0]
            assert blocks[n % NB][1] == expect, (blocks[n % NB], expect)
            bctr[0] += 1
            return get_block(n)

        def layer_norm(gname, bname, final=False):
            pm, pmk = PS[6], ("ps", 6)
            pe2, pe2k = PS[7], ("ps", 7)
            for c in range(8):
                MM(pm[:], ONES, xres[:, c, :], c == 0, c == 7, [("xres", c), "cst"], [pmk])
            for c in range(8):
                sq, sqk = G(8 + (c % 2), 0)
                ACT(sq[:, 0:T], xres[:, c, :], AF.Square, [("xres", c)], [sqk])
                MM(pe2[:], ONES, sq[:, 0:T], c == 0, c == 7, [sqk, "cst"], [pe2k])
            mean, mk = G(8, 1)
            var, vk = G(9, 1)
            CP("dve", mean[:, 0:T], pm[:], [pmk], [mk])
            TT("dve", var[:, 0:T], mean[:, 0:T], mean[:, 0:T], ALU.mult, [mk], [vk])
            TT("dve", var[:, 0:T], pe2[:], var[:, 0:T], ALU.subtract, [pe2k, vk], [vk])
            ACT(var[:, 0:T], var[:, 0:T], AF.Ln, [vk, "misc"], [vk], bias=misc[:, M_EPS:M_EPS + 1], scale=1.0)
            ACT(var[:, 0:T], var[:, 0:T], AF.Exp, [vk], [vk], scale=-0.5)
            for hf in range(4):
                xk = [("xres", c) for c in range(2 * hf, 2 * hf + 2)]
                xs = xres[:, 2 * hf:2 * hf + 2, :]
                TT("dve", xs, xs, mean[:, None, 0:T].broadcast_to([128, 2, T]), ALU.subtract, xk + [mk], xk)
                TT("dve", xs, xs, var[:, None, 0:T].broadcast_to([128, 2, T]), ALU.mult, xk + [vk], xk)
            for c in range(8):
                if final:
                    ob, obk = G(c // 2, c % 2)
                    ACT(ob[:, 0:T], xres[:, c, :], AF.Identity, [("xres", c), "pv"], [obk],
                        bias=pc(bname, c), scale=pc(gname, c))
                    continue
                ACT(xres[:, c, :], xres[:, c, :], AF.Identity, [("xres", c), "pv"], [("xres", c)],
                    bias=pc(bname, c), scale=pc(gname, c))
                CP("dve", xbf[:, c, :], xres[:, c, :], [("xres", c)], [("xbf", c)])

        for it in range(NT):
            t0 = it * T
            for c in range(8):
                S.dma("sp", xres[:, c, :], xT[c * 128:(c + 1) * 128, t0:t0 + T], "xin", writes=[("xres", c)])
            layer_norm("ln_in_g", "ln_in_b")
            for l in range(DEPTH):
                sl = "%d" % l
                uf = [G(0, 0), G(0, 1)]
                hcv = [(GT[1][:, hh, :].bitcast(BF16), ("g", 1, hh)) for hh in range(2)]
                lg = [G(2, 0), G(2, 1)]
                xr = [G(3, 0), G(3, 1)]
                win = dict(sv=None, sk=None)
                side_rot = [0]

                def newps_side():
                    i = 2 + side_rot[0] % 2
                    side_rot[0] += 1
                    return PS[i], ("ps", i)

                def win_chunk(oc, side):
                    if oc % 4 == 0:
                        slot, sk_ = next_block("in")
                        win["sv"] = slot[:].rearrange("p (kc n) -> p kc n", kc=8)
                        win["sk"] = sk_
                    sv, sk = win["sv"], win["sk"]
                    pt, pk = newps_side() if side else newps()
                    co = (oc % 4) * 128
                    for kc in range(8):
                        MM(pt[:], sv[:, kc, co:co + 128], xbf[:, kc, :], kc == 0, kc == 7, [sk, ("xbf", kc)], [pk])
                    bcol = pc("b_in" + sl, oc)
                    ch = oc % 2
                    if oc < 2:
                        ACT(uf[ch][0][:, 0:T], pt[:], AF.Identity, [pk, "pv"], [uf[ch][1]], bias=bcol, scale=1.0)
                        ACT(u_bf[:, ch, :], pt[:], AF.Identity, [pk, "pv"], [("u_bf", ch)], bias=bcol, scale=1.0)
                    elif oc < 4:
                        vt, vk = G(4, ch)
                        ACT(vt[:, 0:T], pt[:], AF.Identity, [pk, "pv"], [vk], bias=bcol, scale=1.0)
                    elif oc < 6:
                        sg, sgk = G(9, ch)
                        vt, vk = G(4, ch)
                        ACT(sg[:, 0:T], pt[:], AF.Sigmoid, [pk, "pv"], [sgk], bias=bcol, scale=1.0)
                        CP("dve", hcv[ch][0][:, 0:30], cvtail[l][:, ch, :], [("cvtail", l)], [hcv[ch][1]])
                        TT("dve", hcv[ch][0][:, 30:30 + T], vt[:, 0:T], sg[:, 0:T], ALU.mult, [vk, sgk], [hcv[ch][1]])
                        CP("dve", cvtail[l][:, ch, :], hcv[ch][0][:, T:T + 30], [hcv[ch][1]], [("cvtail", l)])
                    elif oc < 8:
                        ACT(lg[ch][0][:, 0:T], pt[:], AF.Gelu_apprx_tanh, [pk, "pv"], [lg[ch][1]], bias=bcol, scale=1.0)
                    elif oc < 10:
                        CP("dve", xr[ch][0][:, 0:3], xrtail[l][:, ch, :], [("xrtail", l)], [xr[ch][1]])
                        ACT(xr[ch][0][:, 3:3 + T], pt[:], AF.Identity, [pk, "pv"], [xr[ch][1]], bias=bcol, scale=1.0)
                        CP("dve", xrtail[l][:, ch, :], xr[ch][0][:, T:T + 3], [xr[ch][1]], [("xrtail", l)])
                    else:
                        ACT(q_bf[:, ch, :], pt[:], AF.Identity, [pk, "pv"], [("q_bf", ch)], bias=bcol, scale=1.0)

                win_chunk(0, False)
                win_chunk(1, False)

                def attn_front(hp):
                    po, pok = PS[0], ("ps", 0)
                    pd, pdk = PS[1], ("ps", 1)
                    combos = [(a, mh) for a in range(2) for mh in range(2)]
                    for n, (a, mh) in enumerate(combos):
                        h = 2 * hp + a
                        pt, pk = PS[2 + n % 2], ("ps", 2 + n % 2)
                        MM(pt[:], kT[l][a * 64:(a + 1) * 64, hp, mh * 128:(mh + 1) * 128], q_bf[a * 64:(a + 1) * 64, hp, :],
                           True, True, [("kT", l), ("q_bf", hp)], [pk])
                        eb = e_bf[n % 2]
                        ebk = ("e_bf", n % 2)
                        ACT(eb[:], pt[:], AF.Exp, [pk], [ebk], scale=0.125)
                        MM(po[:], Vp[l][:, h, mh, :], eb[:], n == 0, n == 3, [("Vp", l), ebk], [pok])
                        MM(pd[:], HO[a], eb[:], n == 0, n == 3, ["cst_bf", ebk], [pdk])
                    rc, rck = rcb[hp], ("rcb", hp)
                    ACT(rc[:], pd[:], AF.Ln, [pdk], [rck])
                    ACT(rc[:], rc[:], AF.Exp, [rck], [rck], scale=-1.0)

                def attn_back(hp):
                    rc, rck = rcb[hp], ("rcb", hp)
                    TT("dve", mix[:, 6 + hp, :], PS[0][:], rc[:], ALU.mult, [("ps", 0), rck], [("mix", 6 + hp)])

                def s5_gen():
                    yps = [(PS[6], ("ps", 6)), (PS[7], ("ps", 7))]
                    t1, k1 = G(5, 0)
                    t2, k2 = G(5, 1)
                    t3, k3 = G(6, 0)
                    t4, k4 = G(6, 1)
                    wir, kwir = t1, k1
                    wii, kwii = t3, k3
                    wr, _ = G(7, 0)
                    wi, _ = G(7, 1)
                    b1, kb1 = G(8, 0)
                    b2, kb2 = G(8, 1)
                    stk = ("s5st", l)
                    tk = [("tab", l)]
                    pre, prek = PS[4], ("ps", 4)
                    pim, pimk = PS[5], ("ps", 5)

                    def emit_bu(jj):
                        MM(pre[:], Bl[l][0][:, jj, :], u_bf[:, jj // 4, :], True, True, [("Bl", l), ("u_bf", jj // 4)], [prek])
                        MM(pim[:], Bl[l][1][:, jj, :], u_bf[:, jj // 4, :], True, True, [("Bl", l), ("u_bf", jj // 4)], [pimk])

                    emit_bu(0)
                    for j in range(8):
                        Cj = tabC[l][:, j, :]
                        Sj = tabS[l][:, j, :]
                        for sub in range(T // SQ):
                            cs_ = slice(sub * SQ, (sub + 1) * SQ)
                            kwr, kwi = ("wr", sub), ("wi", sub)
                            TT("dve", t1[:, cs_], pre[:, cs_], Cj, ALU.mult, [prek] + tk, [k1])
                            TT("dve", t2[:, cs_], pim[:, cs_], Sj, ALU.mult, [pimk] + tk, [k2])
                            TT("dve", t3[:, cs_], pim[:, cs_], Cj, ALU.mult, [pimk] + tk, [k3])
                            TT("dve", t4[:, cs_], pre[:, cs_], Sj, ALU.mult, [prek] + tk, [k4])
                            TT("dve", wir[:, cs_], t1[:, cs_], t2[:, cs_], ALU.add, [k1, k2], [kwir])
                            TT("dve", wii[:, cs_], t3[:, cs_], t4[:, cs_], ALU.subtract, [k3, k4], [kwii])
                            if sub == T // SQ - 1 and j < 7:
                                emit_bu(j + 1)
                            yield
                            rb = mc(l, 4 + j).broadcast_to([128, SQ])
                            ini_r, ini_i = stt_s5[l][:, j:j + 1], stt_s5[l][:, 8 + j:9 + j]
                            S.op("dve", lambda e, o=wr[:, cs_], d0=rb, d1=wir[:, cs_], ini=ini_r:
                                 e.tensor_tensor_scan(out=o, data0=d0, data1=d1, initial=ini, op0=ALU.mult, op1=ALU.add),
                                 [kwir, stk, "misc"], [kwr])
                            S.op("dve", lambda e, o=wi[:, cs_], d0=rb, d1=wii[:, cs_], ini=ini_i:
                                 e.tensor_tensor_scan(out=o, data0=d0, data1=d1, initial=ini, op0=ALU.mult, op1=ALU.add),
                                 [kwii, stk, "misc"], [kwi])
                            e0 = (sub + 1) * SQ - 1
                            Cl_, Sl_ = tabC[l][:, j, SQ - 1:SQ], tabS[l][:, j, SQ - 1:SQ]
                            TS("dve", tiny[:, 0:1], wi[:, e0:e0 + 1], Sl_, None, ALU.mult, None, [kwi] + tk, ["tiny"])
                            TS("dve", tiny[:, 1:2], wr[:, e0:e0 + 1], Sl_, None, ALU.mult, None, [kwr] + tk, ["tiny"])
                            STT(stt_s5[l][:, j:j + 1], wr[:, e0:e0 + 1], Cl_, tiny[:, 0:1], ALU.mult, ALU.subtract,
                                [kwr, "tiny"] + tk, [stk])
                            STT(stt_s5[l][:, 8 + j:9 + j], wi[:, e0:e0 + 1], Cl_, tiny[:, 1:2], ALU.mult, ALU.add,
                                [kwi, "tiny"] + tk, [stk])
                            sre_b, sim_b = sre_bf2[j % 2], sim_bf2[j % 2]
                            TT(RB_ENG, b1[:, cs_], wr[:, cs_], Cj, ALU.mult, [kwr] + tk, [kb1])
                            TT(RB_ENG, b2[:, cs_], wi[:, cs_], Sj, ALU.mult, [kwi] + tk, [kb2])
                            TT(RB_ENG, sre_b[:, cs_], b1[:, cs_], b2[:, cs_], ALU.subtract, [kb1, kb2], [("sre_bf", j % 2)])
                            TT(RB_ENG, b1[:, cs_], wi[:, cs_], Cj, ALU.mult, [kwi] + tk, [kb1])
                            TT(RB_ENG, b2[:, cs_], wr[:, cs_], Sj, ALU.mult, [kwr] + tk, [kb2])
                            TT(RB_ENG, sim_b[:, cs_], b1[:, cs_], b2[:, cs_], ALU.add, [kb1, kb2], [("sim_bf", j % 2)])
                            yield
                        m = j // 4
                        MM(yps[m][0][:], Cl[l][0][:, j, :], sre_bf2[j % 2][:], j % 4 == 0, False, [("Cl", l), ("sre_bf", j % 2)], [yps[m][1]])
                        MM(yps[m][0][:], Cl[l][1][:, j, :], sim_bf2[j % 2][:], False, j % 4 == 3, [("Cl", l), ("sim_bf", j % 2)], [yps[m][1]])
                    ygf = [G(5, 1), G(6, 1)]
                    yts = [G(5, 0), G(6, 0)]
                    for m in range(2):
                        STT(yts[m][0][:, 0:T], uf[m][0][:, 0:T], pc("s5_d" + sl, m), yps[m][0][:], ALU.mult, ALU.add,
                            [uf[m][1], "pv", yps[m][1]], [yts[m][1]])
                        ACT(ygf[m][0][:, 0:T], yts[m][0][:, 0:T], AF.Gelu_apprx_tanh, [yts[m][1]], [ygf[m][1]])
                        CP("dve", yg_bf[:, m, :], ygf[m][0][:, 0:T], [ygf[m][1]], [("yg_bf", m)])
                    yield
                    for oc in range(2):
                        pt, pk = PS[4 + oc], ("ps", 4 + oc)
                        for kc in range(2):
                            MM(pt[:], wglu[l][:, kc, oc * 128:(oc + 1) * 128], yg_bf[:, kc, :], kc == 0, kc == 1,
                               [("wglu", l), ("yg_bf", kc)], [pk])
                        ACT(yts[oc][0][:, 0:T], pt[:], AF.Sigmoid, [pk, "pv"], [yts[oc][1]], bias=pc("s5_b_glu" + sl, oc), scale=1.0)
                        TT("dve", mix[:, oc, :], ygf[oc][0][:, 0:T], yts[oc][0][:, 0:T], ALU.mult, [ygf[oc][1], yts[oc][1]],
                           [("mix", oc)])

                def side_gen():
                    for oc in range(2, 12):
                        win_chunk(oc, True)
                        if oc < 10:
                            c = oc - 2
                            ACT(xres[:, c, :], xres[:, c, :], AF.Identity, [("xres", c), "pv"], [("xres", c)],
                                bias=pc("b_out" + sl, c), scale=ALPHA)
                        yield
                    attn_front(0)
                    yield
                    for ch in range(2):
                        if ch == 1:
                            attn_back(0)
                            yield
                            attn_front(1)
                            yield
                        acc, acck = G(4, 0)
                        sq, sqk = G(4, 1)
                        mean, mk = G(9, 0)
                        var, vk = G(9, 1)
                        hb, hk = hcv[ch]
                        pcv, pcvk = newps_side()
                        for k in range(31):
                            dk = ("dg", k % 4)
                            ACT(dg[:, k % 4, :], IDENT, AF.Identity, ["cst_bf", "pv"], [dk], scale=pc("cv_w" + sl, k * 2 + ch))
                            MM(pcv[:], dg[:, k % 4, :], hb[:, k:k + T], k == 0, k == 30, [dk, hk], [pcvk])
                            if k % 8 == 7:
                                yield
                        ACT(acc[:, 0:T], pcv[:], AF.Identity, [pcvk, "pv"], [acck], bias=pc("cv_b" + sl, ch), scale=1.0)
                        ACT(sq[:, 0:T], acc[:, 0:T], AF.Square, [acck], [sqk])
                        pm, pmk = newps_side()
                        pe2, pe2k = newps_side()
                        MM(pm[:], GMAT, acc[:, 0:T], True, True, ["cst", acck], [pmk])
                        MM(pe2[:], GMAT, sq[:, 0:T], True, True, ["cst", sqk], [pe2k])
                        yield
                        CP("dve", mean[:, 0:T], pm[:], [pmk], [mk])
                        TT("dve", var[:, 0:T], mean[:, 0:T], mean[:, 0:T], ALU.mult, [mk], [vk])
                        TT("dve", var[:, 0:T], pe2[:], var[:, 0:T], ALU.subtract, [pe2k, vk], [vk])
                        ACT(var[:, 0:T], var[:, 0:T], AF.Ln, [vk, "misc"], [vk], bias=misc[:, M_EPS:M_EPS + 1], scale=1.0)
                        ACT(var[:, 0:T], var[:, 0:T], AF.Exp, [vk], [vk], scale=-0.5)
                        TT("dve", acc[:, 0:T], acc[:, 0:T], mean[:, 0:T], ALU.subtract, [acck, mk], [acck])
                        yield
                        TT("dve", acc[:, 0:T], acc[:, 0:T], var[:, 0:T], ALU.mult, [acck, vk], [acck])
                        ACT(act_bf[:, ch, :], acc[:, 0:T], AF.Silu, [acck, "pv"], [("act_bf", ch)],
                            bias=pc("cv_gn_b" + sl, ch), scale=pc("cv_gn_g" + sl, ch))
                        yield
                    for oc in range(2):
                        pt, pk = newps_side()
                        for kc in range(2):
                            MM(pt[:], wpw[l][:, kc, oc * 128:(oc + 1) * 128], act_bf[:, kc, :], kc == 0, kc == 1,
                               [("wpw", l), ("act_bf", kc)], [pk])
                        ACT(mix[:, 2 + oc, :], pt[:], AF.Identity, [pk, "pv"], [("mix", 2 + oc)],
                            bias=pc("cv_b_pw" + sl, oc), scale=1.0)
                    for ch in range(2):
                        if ch == 1:
                            attn_back(1)
                            yield
                        xc, xck = G(4, 0)
                        rg, rgk = G(4, 1)
                        ig, igk = G(9, 0)
                        e2, e2k = G(9, 1)
                        xb, xk = xr[ch]
                        ACT(xc[:, 0:T], xb[:, 0:T], AF.Identity, [xk, "pv"], [xck],
                            bias=pc("lru_conv_b" + sl, ch), scale=pc("lru_conv_w" + sl, 0 * 2 + ch))
                        yield
                        for k in range(1, 4):
                            STT(xc[:, 0:T], xb[:, k:k + T], pc("lru_conv_w" + sl, k * 2 + ch), xc[:, 0:T], ALU.mult, ALU.add,
                                [xk, "pv", xck], [xck])
                        CP("act", xc_bf[:, ch, :], xc[:, 0:T], [xck], [("xc_bf", ch)])
                        pr, prk = newps_side()
                        pi_, pik = newps_side()
                        MM(pr[:], wlr[l][:, ch, :], xc_bf[:, ch, :], True, True, [("wlr", l), ("xc_bf", ch)], [prk])
                        MM(pi_[:], wli[l][:, ch, :], xc_bf[:, ch, :], True, True, [("wli", l), ("xc_bf", ch)], [pik])
                        ACT(rg[:, 0:T], pr[:], AF.Sigmoid, [prk, "pv"], [rgk], bias=pc("lru_b_r" + sl, ch), scale=1.0)
                        ACT(ig[:, 0:T], pi_[:], AF.Sigmoid, [pik, "pv"], [igk], bias=pc("lru_b_i" + sl, ch), scale=1.0)
                        ACT(e2[:, 0:T], rg[:, 0:T], AF.Exp, [rgk, "misc"], [e2k], scale=mc(l, 2 + ch))
                        ACT(e2[:, 0:T], e2[:, 0:T], AF.Ln, [e2k, "misc"], [e2k], bias=misc[:, M_ONE:M_ONE + 1], scale=-1.0)
                        ACT(e2[:, 0:T], e2[:, 0:T], AF.Exp, [e2k], [e2k], scale=0.5)
                        yield
                        TT("dve", ig[:, 0:T], ig[:, 0:T], xc[:, 0:T], ALU.mult, [igk, xck], [igk])
                        aa, aak = xc, xck
                        ACT(aa[:, 0:T], rg[:, 0:T], AF.Exp, [rgk, "misc"], [aak], scale=mc(l, ch))
                        TT("dve", ig[:, 0:T], ig[:, 0:T], e2[:, 0:T], ALU.mult, [igk, e2k], [igk])
                        yield
                        S.op("dve", lambda e, o=rg[:, 0:T], d0=aa[:, 0:T], d1=ig[:, 0:T], ini=lru_st[l][:, ch:ch + 1]:
                             e.tensor_tensor_scan(out=o, data0=d0, data1=d1, initial=ini, op0=ALU.mult, op1=ALU.add),
                             [aak, igk, ("lrust", l)], [rgk])
                        CP("dve", lru_st[l][:, ch:ch + 1], rg[:, T - 1:T], [rgk], [("lrust", l)])
                        TT("dve", mix[:, 4 + ch, :], rg[:, 0:T], lg[ch][0][:, 0:T], ALU.mult, [rgk, lg[ch][1]], [("mix", 4 + ch)])
                        yield

                gens = [s5_gen(), side_gen()]
                alive = [True, True]
                while any(alive):
                    for gi in range(2):
                        if alive[gi]:
                            try:
                                next(gens[gi])
                            except StopIteration:
                                alive[gi] = False

                for oc in range(8):
                    if oc % 4 == 0:
                        slot, sk = next_block("out")
                        sv = slot[:].rearrange("p (kc n) -> p kc n", kc=8)
                    pt, pk = newps()
                    co = (oc % 4) * 128
                    for kc in range(8):
                        MM(pt[:], sv[:, kc, co:co + 128], mix[:, kc, :], kc == 0, kc == 7, [sk, ("mix", kc)], [pk])
                    TT("dve", xres[:, oc, :], pt[:], xres[:, oc, :], ALU.add, [pk, ("xres", oc)], [("xres", oc)])
                layer_norm("ln1_g" + sl, "ln1_b" + sl)

                def ffn_finish(j, par, res):
                    gg, ggk = G(par * 3 + 2, 0)
                    ACT(gg[:, 0:T], res[1][0][:, 0:T], AF.Gelu_apprx_tanh, [res[1][1]], [ggk])
                    TT("dve", hff[:, j, :], res[0][0][:, 0:T], gg[:, 0:T], ALU.mult, [res[0][1], ggk], [("hff", j)])
                pend = None
                for j in range(NPAIR):
                    if j % 2 == 0:
                        slot, sk = next_block("up")
                        sv = slot[:].rearrange("p (kc n) -> p kc n", kc=8)
                    s_ = j % 2
                    par = j % 3
                    res = []
                    pts = []
                    for vg in range(2):
                        pt, pk = newps()
                        co = vg * 256 + s_ * 128
                        for kc in range(8):
                            MM(pt[:], sv[:, kc, co:co + 128], xbf[:, kc, :], kc == 0, kc == 7, [sk, ("xbf", kc)], [pk])
                        pts.append((pt, pk))
                    for vg in range(2):
                        cidx = vg * NPAIR + j
                        pt, pk = pts[vg]
                        ub, ubk = G(par * 3 + vg, 0)
                        yv, yvk = G(par * 3 + vg, 1)
                        utk = ("utail", l, cidx)
                        CP("act", ub[:, 0:2], utail[l][:, cidx, :], [utk], [ubk])
                        CP("act", ub[:, 2:2 + T], pt[:], [pk], [ubk])
                        ACT(yv[:, 0:T], pt[:], AF.Identity, [pk, "pv"], [yvk],
                            bias=pc("ffn_conv_b" + sl, cidx), scale=pc("ffn_conv_w" + sl, 2 * 2 * NPAIR + cidx))
                        CP("act", utail[l][:, cidx, :], pt[:, T - 2:T], [pk], [utk])
                        res.append((yv, yvk, ub, ubk, cidx))
                    for k in (1, 0):
                        for vg in range(2):
                            yv, yvk, ub, ubk, cidx = res[vg]
                            STT(yv[:, 0:T], ub[:, k:k + T], pc("ffn_conv_w" + sl, k * 2 * NPAIR + cidx), yv[:, 0:T],
                                ALU.mult, ALU.add, [ubk, "pv", yvk], [yvk])
                    if pend is not None:
                        ffn_finish(*pend)
                    pend = (j, par, res)
                ffn_finish(*pend)
                dps = [(PS[i], ("ps", i)) for i in range(8)]
                for bi in range(6):
                    slot, sk = next_block("dn")
                    nk = 4 if bi < 5 else 2
                    sv = slot[:, 0:nk * 1024].rearrange("p (kc n) -> p kc n", kc=nk)
                    for oc in range(8):
                        for kk in range(nk):
                            kc = bi * 4 + kk
                            MM(dps[oc][0][:], sv[:, kk, oc * 128:(oc + 1) * 128], hff[:, kc, :], kc == 0, kc == NPAIR - 1,
                               [sk, ("hff", kc)], [dps[oc][1]])
                for oc in range(8):
                    STT(xres[:, oc, :], xres[:, oc, :], ALPHA, dps[oc][0][:], ALU.mult, ALU.add,
                        [("xres", oc), dps[oc][1]], [("xres", oc)])
                layer_norm("ln2_g" + sl, "ln2_b" + sl, final=(l == DEPTH - 1))
            for c in range(8):
                ob, obk = G(c // 2, c % 2)
                S.dma("sp", outT[c * 128:(c + 1) * 128, t0:t0 + T], ob[:, 0:T], "xout", reads=[obk])
        S.emit(final_waits=[("sp", ("d", "xout", S.dma_cnt["xout"]))])
    return nc


def _consts():
    c = np.zeros((128, 256 + SQ), np.float32)
    c[:, 0:128] = 1.0 / D
    for g in range(2):
        c[g * 64:(g + 1) * 64, 128 + g * 64:128 + (g + 1) * 64] = 1.0 / 64
    c[:, 256:256 + SQ] = np.arange(1, SQ + 1, dtype=np.float32)[None, :]
    cb = np.zeros((128, 384), np.float32)
    cb[:, 0:64] = 1.0
    cb[:, 128 + 64:256] = 1.0
    cb[:, 256:384] = np.eye(128, dtype=np.float32)
    return c, cb


def prepare(inputs, NT):
    inp = {k: np.asarray(v) for k, v in inputs.items()}
    pvec, off = _pvec(inp)
    c_, cb_ = _consts()
    common = {"pvec": pvec, "cst": c_, "cstb": cb_}
    for l in range(DEPTH):
        s = "%d" % l
        common["w_in" + s] = np.ascontiguousarray(inp["w_in"][l], np.float32)
        common["w_out" + s] = np.ascontiguousarray(inp["w_out"][l], np.float32)
        common["w_up" + s] = np.ascontiguousarray(inp["ffn_w_up"][l], np.float32)
        common["w_dn" + s] = np.ascontiguousarray(inp["ffn_w_down"][l], np.float32)
        common["w_kv" + s] = np.ascontiguousarray(inp["attn_w_kv"][l], np.float32)
        common["w_glu" + s] = np.ascontiguousarray(inp["s5_w_glu"][l], np.float32)
        common["w_pw" + s] = np.ascontiguousarray(inp["cv_w_pw"][l], np.float32)
        common["lru_r" + s] = _lru_bd(inp["lru_w_r"][l])
        common["lru_i" + s] = _lru_bd(inp["lru_w_i"][l])
        frep, bt, ct = _s5_layouts(inp, l)
        common["frep" + s] = frep
        common["bt" + s] = bt
        common["ct" + s] = ct
    maps = []
    for b in range(inp["x"].shape[0]):
        m = dict(common)
        m["xT"] = np.ascontiguousarray(inp["x"][b, :NT * T, :].T, np.float32)
        m["memT"] = np.ascontiguousarray(inp["mem"][b].T, np.float32)
        maps.append(m)
    return maps, off, pvec.shape[1]


def run(inputs, NT):
    maps, off, npv = prepare(inputs, NT)
    nc = build(NT, off, npv)
    res = run_bass_kernel_spmd(nc, maps, core_ids=list(range(len(maps))))
    out = np.stack([np.ascontiguousarray(r["outT"].T) for r in res.results], axis=0)
    return out.astype(np.float32)


def kernel(**inputs):
    return run(inputs, SEQ // T)
```
